# Optimizing a Trainium2 kernel written in Bass

```python
import jax, jax.numpy as jnp
from jax import lax
import numpy as np

D_MODEL = 4096
BATCH = 1
SEQ = 8192
DEPTH = 2

GRID_W = 64
CTX_LEN = 256
NA_HEADS = 16
HEAD_DIM = 128
NA_WIN_H = 8
NA_WIN_W = 16
NA_WIDTH = NA_HEADS * HEAD_DIM
MLA_HEADS = 16
MLA_Q_RANK = 1536
MLA_KV_RANK = 512
MLA_NOPE = 128
MLA_ROPE = 64
MLA_V = 128
MLA_QK = MLA_NOPE + MLA_ROPE
MLA_WIDTH = MLA_HEADS * MLA_V
OFF_NA_K = NA_WIDTH
OFF_Q_LAT = 3 * NA_WIDTH
OFF_KV_LAT = OFF_Q_LAT + MLA_Q_RANK
IN_WIDTH = OFF_KV_LAT + MLA_KV_RANK + MLA_ROPE
CONV_WIDTH = 31
D_FF = 4 * D_MODEL
ROPE_THETA = 10000.0
NORM_EPS = 1e-6
LN_EPS = 1e-5
Q_BLOCK = 128
N_ATT_LAYERS = (DEPTH + 1) // 2
N_CONV_LAYERS = DEPTH // 2

kernel_name = 'hybrid_na_mla_conformer_dit_block'


def rms_norm(x, g):
    xf = x.astype(jnp.float32)
    y = xf * lax.rsqrt(jnp.mean(jnp.square(xf), axis=-1, keepdims=True) + NORM_EPS)
    return (y * g.astype(jnp.float32)).astype(x.dtype)


def layer_norm(x, g, b):
    xf = x.astype(jnp.float32)
    mu = jnp.mean(xf, axis=-1, keepdims=True)
    var = jnp.mean(jnp.square(xf - mu), axis=-1, keepdims=True)
    y = (xf - mu) * lax.rsqrt(var + LN_EPS) * g.astype(jnp.float32) + b.astype(jnp.float32)
    return y.astype(x.dtype)


def modulate(h, shift, scale):
    return h * (1 + scale) + shift


def axial_rope(n_tok):
    t = jnp.arange(n_tok, dtype=jnp.int32)
    pos = jnp.stack([t // GRID_W, t % GRID_W], axis=-1).astype(jnp.float32)
    n_freq = MLA_ROPE // 4
    inv_freq = ROPE_THETA ** (-jnp.arange(n_freq, dtype=jnp.float32) / n_freq)
    ang = (pos[:, :, None] * inv_freq).reshape(n_tok, 2 * n_freq)
    return jnp.cos(ang), jnp.sin(ang)


def rotate_tail(t, rope):
    cos, sin = rope
    c = cos[:, None, :]
    s = sin[:, None, :]
    half = MLA_ROPE // 2
    head, x1, x2 = t[..., :-MLA_ROPE], t[..., -MLA_ROPE:-half], t[..., -half:]
    return jnp.concatenate([head, (x1 * c - x2 * s).astype(t.dtype), (x1 * s + x2 * c).astype(t.dtype)], axis=-1)


def dense_block_attention(q, k, v, scale):
    B, S, H, Dq = q.shape
    qb = jnp.moveaxis(q.reshape(B, S // Q_BLOCK, Q_BLOCK, H, Dq), 1, 0)

    def one_block(q_blk):
        s = jnp.einsum('bqhd,bkhd->bhqk', q_blk, k).astype(jnp.float32) * scale
        p = jax.nn.softmax(s, axis=-1).astype(v.dtype)
        return jnp.einsum('bhqk,bkhd->bqhd', p, v)

    out = lax.map(one_block, qb)
    return jnp.moveaxis(out, 0, 1).reshape(B, S, H, v.shape[-1])


def neighbourhood_attention(q, k, v, k_ctx, v_ctx, rpb):
    B, S, H, Dh = q.shape
    rows = S // GRID_W
    kh = min(NA_WIN_H, rows)
    n_key = kh * GRID_W
    scale = Dh ** -0.5
    qg = q.reshape(B, rows, GRID_W, H, Dh)
    kg = k.reshape(B, rows, GRID_W, H, Dh)
    vg = v.reshape(B, rows, GRID_W, H, Dh)
    q_col = jnp.arange(GRID_W)
    c_start = jnp.clip(q_col - NA_WIN_W // 2, 0, GRID_W - NA_WIN_W)
    k_col = jnp.tile(jnp.arange(GRID_W), kh)
    k_band_row = jnp.repeat(jnp.arange(kh), GRID_W)
    in_window = (k_col[None, :] >= c_start[:, None]) & (k_col[None, :] < c_start[:, None] + NA_WIN_W)
    col_idx = jnp.clip(k_col[None, :] - q_col[:, None], 1 - NA_WIN_W, NA_WIN_W - 1) + (NA_WIN_W - 1)

    def one_row(r):
        r_start = jnp.clip(r - kh // 2, 0, rows - kh)
        q_r = lax.dynamic_index_in_dim(qg, r, axis=1, keepdims=False)
        k_r = lax.dynamic_slice_in_dim(kg, r_start, kh, axis=1).reshape(B, n_key, H, Dh)
        v_r = lax.dynamic_slice_in_dim(vg, r_start, kh, axis=1).reshape(B, n_key, H, Dh)
        row_idx = r_start + k_band_row - r + (NA_WIN_H - 1)
        bias = rpb[:, row_idx[None, :], col_idx].astype(jnp.float32)
        s_win = jnp.einsum('bqhd,bkhd->bhqk', q_r, k_r).astype(jnp.float32) * scale + bias
        s_win = jnp.where(in_window, s_win, -jnp.inf)
        s_ctx = jnp.einsum('bqhd,bchd->bhqc', q_r, k_ctx).astype(jnp.float32) * scale
        p = jax.nn.softmax(jnp.concatenate([s_win, s_ctx], axis=-1), axis=-1).astype(v.dtype)
        return (jnp.einsum('bhqk,bkhd->bqhd', p[..., :n_key], v_r)
                + jnp.einsum('bhqc,bchd->bqhd', p[..., n_key:], v_ctx))

    out = lax.map(one_row, jnp.arange(rows))
    return jnp.moveaxis(out, 0, 1).reshape(B, S, H, Dh)


def mla_heads(q_lat, kv_lat, k_rope, g_qa, w_qb, g_kva, w_kvb, g_q, g_k, rope):
    B, N, _ = kv_lat.shape
    kv = (rms_norm(kv_lat, g_kva) @ w_kvb).reshape(B, N, MLA_HEADS, MLA_NOPE + MLA_V)
    k_nope, v = kv[..., :MLA_NOPE], kv[..., MLA_NOPE:]
    k_pe = jnp.broadcast_to(k_rope[:, :, None, :], (B, N, MLA_HEADS, MLA_ROPE))
    k = rms_norm(jnp.concatenate([k_nope, k_pe], axis=-1), g_k)
    q = None
    if q_lat is not None:
        q = rms_norm((rms_norm(q_lat, g_qa) @ w_qb).reshape(B, N, MLA_HEADS, MLA_QK), g_q)
    if rope is not None:
        k = rotate_tail(k, rope)
        q = rotate_tail(q, rope)
    return q, k, v


def attention_mixer(h_lat, h_ctx, ctx_queries, rope, w_in, g_qa, w_qb, g_kva, w_kvb,
                    g_na_q, g_na_k, na_rpb, g_mla_q, g_mla_k, w_out):
    def project(h, with_q, rope_tab):
        B, N, _ = h.shape
        heads = lambda t: t.reshape(B, N, NA_HEADS, HEAD_DIM)
        if with_q:
            z = h @ w_in
            na_q = rms_norm(heads(z[..., :OFF_NA_K]), g_na_q)
            na_kv = z[..., OFF_NA_K:OFF_Q_LAT]
            q_lat = z[..., OFF_Q_LAT:OFF_KV_LAT]
            kv_rest = z[..., OFF_KV_LAT:]
        else:
            na_q = q_lat = None
            na_kv = h @ w_in[:, OFF_NA_K:OFF_Q_LAT]
            kv_rest = h @ w_in[:, OFF_KV_LAT:]
        na_k = rms_norm(heads(na_kv[..., :NA_WIDTH]), g_na_k)
        na_v = heads(na_kv[..., NA_WIDTH:])
        m_q, m_k, m_v = mla_heads(q_lat, kv_rest[..., :MLA_KV_RANK], kv_rest[..., MLA_KV_RANK:],
                                  g_qa, w_qb, g_kva, w_kvb, g_mla_q, g_mla_k, rope_tab)
        return na_q, na_k, na_v, m_q, m_k, m_v

    B, S, _ = h_lat.shape
    nq, nk, nv, mq, mk, mv = project(h_lat, True, rope)
    cnq, cnk, cnv, cmq, cmk, cmv = project(h_ctx, ctx_queries, None)
    o_na = neighbourhood_attention(nq, nk, nv, cnk, cnv, na_rpb)
    o_mla = dense_block_attention(mq, jnp.concatenate([cmk, mk], axis=1),
                                  jnp.concatenate([cmv, mv], axis=1), MLA_QK ** -0.5)
    y_lat = jnp.concatenate([o_na.reshape(B, S, NA_WIDTH), o_mla.reshape(B, S, MLA_WIDTH)], axis=-1) @ w_out
    y_ctx = None
    if ctx_queries:
        C = h_ctx.shape[1]
        o_cna = dense_block_attention(cnq, cnk, cnv, HEAD_DIM ** -0.5)
        o_cmla = dense_block_attention(cmq, cmk, cmv, MLA_QK ** -0.5)
        y_ctx = jnp.concatenate([o_cna.reshape(B, C, NA_WIDTH), o_cmla.reshape(B, C, MLA_WIDTH)], axis=-1) @ w_out
    return y_lat, y_ctx


def conformer_conv(h, w_pw1, b_pw1, w_dw, b_dw, g_ln, b_ln, w_pw2, b_pw2):
    D = h.shape[-1]
    a, gate = jnp.split(h @ w_pw1 + b_pw1, 2, axis=-1)
    u = a * jax.nn.sigmoid(gate)
    pad = CONV_WIDTH // 2
    u = lax.conv_general_dilated(u, w_dw[:, None, :], window_strides=(1,), padding=[(pad, pad)],
                                 dimension_numbers=('NWC', 'WIO', 'NWC'), feature_group_count=D) + b_dw
    u = layer_norm(u, g_ln, b_ln)
    return jax.nn.silu(u) @ w_pw2 + b_pw2


def squared_relu_mlp(h, w1, w2):
    return jnp.square(jax.nn.relu(h @ w1)) @ w2


def setup_inputs(seed: int = 0) -> dict:
    key = jax.random.key(seed)
    ks = list(jax.random.split(key, 40))
    nrm = lambda shape, s: jax.random.normal(ks.pop(), shape, jnp.float32) * s
    gain = lambda shape: 1.0 + nrm(shape, 0.02)
    D = D_MODEL
    NA = N_ATT_LAYERS
    NC = N_CONV_LAYERS
    return {
        'x': nrm((BATCH, SEQ, D), 1.0),
        'c': nrm((BATCH, D), 1.0),
        'ctx': nrm((BATCH, CTX_LEN, D), 1.0),
        'c_ctx': nrm((D,), 1.0),
        'ada_w': nrm((DEPTH, D, 6 * D), 0.5 * D ** -0.5),
        'ada_b': nrm((DEPTH, 6 * D), 0.01),
        'g_mix': gain((DEPTH, D)),
        'g_mlp': gain((DEPTH, D)),
        'mlp_w1': nrm((DEPTH, D, D_FF), D ** -0.5),
        'mlp_w2': nrm((DEPTH, D_FF, D), D_FF ** -0.5),
        'att_w_in': nrm((NA, D, IN_WIDTH), D ** -0.5),
        'att_g_qa': gain((NA, MLA_Q_RANK)),
        'att_w_qb': nrm((NA, MLA_Q_RANK, MLA_HEADS * MLA_QK), MLA_Q_RANK ** -0.5),
        'att_g_kva': gain((NA, MLA_KV_RANK)),
        'att_w_kvb': nrm((NA, MLA_KV_RANK, MLA_HEADS * (MLA_NOPE + MLA_V)), MLA_KV_RANK ** -0.5),
        'att_g_na_q': gain((NA, HEAD_DIM)),
        'att_g_na_k': gain((NA, HEAD_DIM)),
        'att_na_rpb': nrm((NA, NA_HEADS, 2 * NA_WIN_H - 1, 2 * NA_WIN_W - 1), 0.1),
        'att_g_mla_q': gain((NA, MLA_QK)),
        'att_g_mla_k': gain((NA, MLA_QK)),
        'att_w_out': nrm((NA, NA_WIDTH + MLA_WIDTH, D), (NA_WIDTH + MLA_WIDTH) ** -0.5),
        'conv_w_pw1': nrm((NC, D, 2 * D), D ** -0.5),
        'conv_b_pw1': nrm((NC, 2 * D), 0.01),
        'conv_w_dw': nrm((NC, CONV_WIDTH, D), CONV_WIDTH ** -0.5),
        'conv_b_dw': nrm((NC, D), 0.01),
        'conv_g_ln': gain((NC, D)),
        'conv_b_ln': nrm((NC, D), 0.01),
        'conv_w_pw2': nrm((NC, D, D), D ** -0.5),
        'conv_b_pw2': nrm((NC, D), 0.01),
    }


def reference(x, c, ctx, c_ctx, ada_w, ada_b, g_mix, g_mlp, mlp_w1, mlp_w2,
              att_w_in, att_g_qa, att_w_qb, att_g_kva, att_w_kvb, att_g_na_q, att_g_na_k,
              att_na_rpb, att_g_mla_q, att_g_mla_k, att_w_out,
              conv_w_pw1, conv_b_pw1, conv_w_dw, conv_b_dw, conv_g_ln, conv_b_ln,
              conv_w_pw2, conv_b_pw2):
    rope = axial_rope(x.shape[1])
    sc_lat = jax.nn.silu(c)
    sc_ctx = jax.nn.silu(c_ctx)[None]
    xc = ctx
    for layer in range(DEPTH):
        is_att = layer % 2 == 0
        j = layer // 2
        upd_ctx = any(m % 2 == 0 for m in range(layer + 1, DEPTH))
        mod = jnp.split((sc_lat @ ada_w[layer] + ada_b[layer])[:, None, :], 6, axis=-1)
        hl = modulate(rms_norm(x, g_mix[layer]), mod[0], mod[1])
        if is_att or upd_ctx:
            cmod = jnp.split((sc_ctx @ ada_w[layer] + ada_b[layer])[:, None, :], 6, axis=-1)
            hc = modulate(rms_norm(xc, g_mix[layer]), cmod[0], cmod[1])
        if is_att:
            yl, yc = attention_mixer(hl, hc, upd_ctx, rope, att_w_in[j], att_g_qa[j], att_w_qb[j],
                                     att_g_kva[j], att_w_kvb[j], att_g_na_q[j], att_g_na_k[j],
                                     att_na_rpb[j], att_g_mla_q[j], att_g_mla_k[j], att_w_out[j])
        else:
            conv_p = (conv_w_pw1[j], conv_b_pw1[j], conv_w_dw[j], conv_b_dw[j],
                      conv_g_ln[j], conv_b_ln[j], conv_w_pw2[j], conv_b_pw2[j])
            yl = conformer_conv(hl, *conv_p)
            yc = conformer_conv(hc, *conv_p) if upd_ctx else None
        x = x + mod[2] * yl
        x = x + mod[5] * squared_relu_mlp(modulate(rms_norm(x, g_mlp[layer]), mod[3], mod[4]),
                                          mlp_w1[layer], mlp_w2[layer])
        if upd_ctx:
            xc = xc + cmod[2] * yc
            xc = xc + cmod[5] * squared_relu_mlp(modulate(rms_norm(xc, g_mlp[layer]), cmod[3], cmod[4]),
                                                 mlp_w1[layer], mlp_w2[layer])
    return x
```

```python
import numpy as np
from contextlib import ExitStack
import concourse.bass as bass
import concourse.mybir as mybir
from concourse.bass_utils import run_bass_kernel_spmd

F32 = mybir.dt.float32
BF16 = mybir.dt.bfloat16
AF = mybir.ActivationFunctionType
ALU = mybir.AluOpType
AX = mybir.AxisListType


class Buf:
    def __init__(self, name, ap):
        self.name = name
        self.ap = ap
        self.st = {}

    def __getitem__(self, idx):
        return self.ap[idx]


class Sched:
    ENG = ("pe", "act", "dve", "pool", "sp")

    def __init__(self, nc, stack):
        self.nc = nc
        self.stack = stack
        self.ops = {e: [] for e in self.ENG}
        self.sem = {e: stack.enter_context(nc.semaphore("ms_" + e)) for e in self.ENG}
        self.cnt = {e: 0 for e in self.ENG}
        self.NDS = 20
        self.dsem = {q: [stack.enter_context(nc.semaphore(f"dq_{q}_{i}")) for i in range(self.NDS)]
                     for q in ("sp", "pool", "act")}
        self.dcnt = {q: [0] * self.NDS for q in ("sp", "pool", "act")}
        self.drr = {q: 0 for q in ("sp", "pool", "act")}
        self.waited = {e: {} for e in self.ENG}
        self.nbuf = 0

    def sbuf(self, name, shape, dtype, stack=None):
        self.nbuf += 1
        t = (stack or self.stack).enter_context(self.nc.sbuf_tensor(f"{name}_{self.nbuf}", list(shape), dtype))
        return Buf(name, t)

    def psum(self, name, shape, dtype, stack=None):
        self.nbuf += 1
        t = (stack or self.stack).enter_context(self.nc.psum_tensor(f"{name}_{self.nbuf}", list(shape), dtype))
        return Buf(name, t)

    def dram(self, name, shape, dtype, kind="Internal"):
        t = self.nc.dram_tensor(name, list(shape), dtype, kind=kind)
        b = Buf(name, t.ap())
        b.is_dram_out = True
        return b

    def _deps(self, reads, writes):
        toks = []
        for (b, k) in reads:
            for kk in self._keys(b, k):
                s = b.st.get(kk)
                if s and s[0] is not None:
                    toks.append(s[0])
        for (b, k) in writes:
            for kk in self._keys(b, k):
                s = b.st.get(kk)
                if s:
                    if s[0] is not None:
                        toks.append(s[0])
                    toks.extend(s[1].values())
        return toks

    @staticmethod
    def _keys(b, k):
        if k is None:
            return list(b.st.keys()) + ([None] if None not in b.st else [])
        return [k, None]

    def _commit(self, reads, writes, tok):
        for (b, k) in reads:
            s = b.st.setdefault(k, [None, {}])
            old = s[1].get(tok[2])
            if old is None or old[1] < tok[1]:
                s[1][tok[2]] = tok
        for (b, k) in writes:
            if k is None:
                b.st = {None: [tok, {}]}
            else:
                b.st[k] = [tok, {}]

    def _emit_waits(self, eng, toks):
        need = {}
        for (sem, val, sid) in toks:
            if eng == "pe" and sid == "ms_pe":
                continue
            if self.waited[eng].get(sid, 0) >= val:
                continue
            if sid not in need or need[sid][1] < val:
                need[sid] = (sem, val)
        for sid, (sem, val) in need.items():
            self.waited[eng][sid] = val
            self.ops[eng].append(("wait", sem, val))

    @staticmethod
    def _norm(lst):
        out = []
        for x in lst or []:
            if isinstance(x, Buf):
                out.append((x, None))
            else:
                out.append(x)
        return out

    def op(self, eng, fn, reads=None, writes=None, inc=True):
        reads = self._norm(reads)
        writes = self._norm(writes)
        toks = self._deps(reads, writes)
        self._emit_waits(eng, toks)
        if inc:
            self.cnt[eng] += 1
            tok = (self.sem[eng], self.cnt[eng], "ms_" + eng)
            self.ops[eng].append(("op", fn, self.sem[eng]))
            self._commit(reads, writes, tok)
        else:
            tok = (self.sem[eng], self.cnt[eng] + 1, "ms_" + eng)
            self.ops[eng].append(("op", fn, None))
            self._commit(reads, writes, tok)
        return

    def dma(self, q, out_ap, in_ap, reads=None, writes=None, **kw):
        reads = self._norm(reads)
        writes = self._norm(writes)
        toks = self._deps(reads, writes)
        self._emit_waits(q, toks)
        i = self.drr[q]
        self.drr[q] = (i + 1) % self.NDS
        self.dcnt[q][i] += 16
        sem = self.dsem[q][i]
        tok = (sem, self.dcnt[q][i], f"dq_{q}_{i}")
        self.ops[q].append(("dma", out_ap, in_ap, sem, kw))
        self._commit(reads, writes, tok)

    def barrier(self):
        toks = [(self.sem[e], self.cnt[e], "ms_" + e) for e in self.ENG if self.cnt[e] > 0]
        for q in self.dsem:
            for i in range(self.NDS):
                if self.dcnt[q][i] > 0:
                    toks.append((self.dsem[q][i], self.dcnt[q][i], f"dq_{q}_{i}"))
        for e in self.ENG:
            need = []
            for (sem, val, sid) in toks:
                if self.waited[e].get(sid, 0) >= val:
                    continue
                self.waited[e][sid] = val
                self.ops[e].append(("wait", sem, val))

    def wait_all(self, eng, bufs):
        toks = []
        for b in bufs:
            for k, s in b.st.items():
                if s[0] is not None:
                    toks.append(s[0])
        self._emit_waits(eng, toks)

    def replay(self):
        nc = self.nc
        engmap = {"pe": "tensor", "act": "scalar", "dve": "vector", "pool": "gpsimd", "sp": "sync"}
        with nc.Block() as block:
            for e in self.ENG:
                ops = self.ops[e]

                def body(h, ops=ops):
                    for o in ops:
                        if o[0] == "wait":
                            h.wait_ge(o[1], o[2])
                        elif o[0] == "op":
                            ins = o[1](h)
                            if o[2] is not None:
                                ins.then_inc(o[2], 1)
                        else:
                            h.dma_start(out=o[1], in_=o[2], **o[4]).then_inc(o[3], 16)

                getattr(block, engmap[e])(body)
        self.ops = {e: [] for e in self.ENG}


class Cfg:
    def __init__(self, D=4096, SEQ=8192, NCORES=8, CTX=256, NAH=16, MLAH=16, QR=1536, KVR=512, DFF=16384):
        self.D, self.SEQ, self.NCORES, self.CTX = D, SEQ, NCORES, CTX
        self.NAH, self.MLAH, self.QR, self.KVR, self.DFF = NAH, MLAH, QR, KVR, DFF
        self.L = 2
        self.W = 64
        self.ROWS = SEQ // 64
        self.RPC = self.ROWS // NCORES
        assert self.RPC == 16
        self.DC = D // 128
        self.TQ = (self.RPC + 2) * 64
        self.TKW = (self.RPC + 10) * 64
        self.TK = self.TKW + CTX
        self.TA = SEQ + CTX
        self.QOFF = 4 * 64
        self.OWN = 64
        self.NAW = NAH * 128
        self.MLAW = MLAH * 128
        self.OFF_Q_LAT = 3 * self.NAW
        self.OFF_KV_LAT = self.OFF_Q_LAT + QR
        self.INW = self.OFF_KV_LAT + KVR + 64
        self.QC = QR // 128
        self.KC = KVR // 128
        self.FC = DFF // 128
        self.NT = 384
        self.CONVW = 31
        self.G = min(16, self.FC)
        self.EPS = 1e-6
        self.LN_EPS = 1e-5


def rope_tables(n_tok, W=64):
    t = np.arange(n_tok, dtype=np.int32)
    pos = np.stack([t // W, t % W], axis=-1).astype(np.float32)
    n_freq = 16
    inv_freq = (np.float32(10000.0) ** (-np.arange(n_freq, dtype=np.float32) / np.float32(n_freq))).astype(np.float32)
    ang = (pos[:, :, None] * inv_freq).reshape(n_tok, 2 * n_freq).astype(np.float32)
    return np.cos(ang).astype(np.float32), np.sin(ang).astype(np.float32)


def lay_pc(v, nchunk):
    return np.ascontiguousarray(np.asarray(v, np.float32).reshape(nchunk, 128).T)


def prep_inputs(C, inp):
    D, SEQ = C.D, C.SEQ
    x = np.asarray(inp["x"], np.float32).reshape(SEQ, D)
    ctx = np.asarray(inp["ctx"], np.float32).reshape(C.CTX, D)
    cos, sin = rope_tables(SEQ)
    cos2 = np.concatenate([cos.T, cos.T], 0)
    sin2 = np.concatenate([sin.T, sin.T], 0)
    cosk = np.concatenate([cos2, np.ones((64, C.CTX), np.float32)], 1)
    sink = np.concatenate([sin2, np.zeros((64, C.CTX), np.float32)], 1)
    rmat = np.zeros((64, 64), np.float32)
    for i in range(32):
        rmat[i + 32, i] = -1.0
        rmat[i, i + 32] = 1.0
    ident = np.eye(128, dtype=np.float32)
    sel = np.zeros((2, 2, 128), np.float32)
    sel[0, 0, :] = 1.0
    sel[1, 1, :] = 1.0
    rpb = np.asarray(inp["att_na_rpb"], np.float32)[0]
    gq = np.asarray(inp["att_g_mla_q"], np.float32)[0]
    gk = np.asarray(inp["att_g_mla_k"], np.float32)[0]
    shared = {
        "x_allT": np.ascontiguousarray(x.T), "ctxT": np.ascontiguousarray(ctx.T),
        "g_mix_c": np.ascontiguousarray(np.concatenate([lay_pc(np.asarray(inp["g_mix"], np.float32)[l], C.DC) for l in range(C.L)], 1)),
        "c2_l": np.ascontiguousarray(np.stack([lay_pc(np.asarray(inp["c"]).reshape(-1), C.DC),
                                               lay_pc(np.asarray(inp["c_ctx"]).reshape(-1), C.DC)], -1)).reshape(128, C.DC * 2),
        "ada_w": np.asarray(inp["ada_w"], np.float32).reshape(C.L * D, 6 * D),
        "ada_b": np.asarray(inp["ada_b"], np.float32).reshape(C.L, 6 * D),
        "g_mix": np.asarray(inp["g_mix"], np.float32).reshape(C.L, D),
        "g_mlp": np.asarray(inp["g_mlp"], np.float32).reshape(C.L, D),
        "mlp_w1": np.asarray(inp["mlp_w1"], np.float32).reshape(C.L * D, C.DFF),
        "mlp_w2": np.asarray(inp["mlp_w2"], np.float32).reshape(C.L * C.DFF, D),
        "w_in": np.asarray(inp["att_w_in"], np.float32).reshape(D, C.INW),
        "g_qa_l": lay_pc(inp["att_g_qa"][0], C.QC),
        "w_qb": np.asarray(inp["att_w_qb"], np.float32).reshape(C.QR, C.MLAH * 192),
        "g_kva_l": lay_pc(inp["att_g_kva"][0], C.KC),
        "w_kvb": np.asarray(inp["att_w_kvb"], np.float32).reshape(C.KVR, C.MLAH * 256),
        "g_na_q": np.asarray(inp["att_g_na_q"], np.float32).reshape(128, 1),
        "g_na_k": np.asarray(inp["att_g_na_k"], np.float32).reshape(128, 1),
        "g_mq_n": gq[:128].reshape(128, 1).copy(), "g_mq_r": gq[128:].reshape(64, 1).copy(),
        "g_mk_n": gk[:128].reshape(128, 1).copy(), "g_mk_r": gk[128:].reshape(64, 1).copy(),
        "w_out": np.asarray(inp["att_w_out"], np.float32).reshape(C.NAW + C.MLAW, D),
        "w_pw1": np.asarray(inp["conv_w_pw1"], np.float32).reshape(D, 2 * D),
        "b_pw1_l": lay_pc(inp["conv_b_pw1"][0], 2 * C.DC),
        "w_dw_l": np.ascontiguousarray(np.asarray(inp["conv_w_dw"], np.float32)[0].T.reshape(C.DC, 128, C.CONVW).transpose(1, 0, 2)),
        "b_dw_l": lay_pc(inp["conv_b_dw"][0], C.DC),
        "g_ln_l": lay_pc(inp["conv_g_ln"][0], C.DC),
        "b_ln_l": lay_pc(inp["conv_b_ln"][0], C.DC),
        "w_pw2": np.asarray(inp["conv_w_pw2"], np.float32).reshape(D, D),
        "b_pw2": np.asarray(inp["conv_b_pw2"], np.float32).reshape(1, D),
        "cosk": cosk, "sink": sink, "rmat": rmat, "ident": ident, "sel": sel.reshape(4, 128),
    }
    maps = []
    for ci in range(C.NCORES):
        r0 = ci * C.RPC
        m = dict(shared)
        xkv = np.zeros((C.TK, D), np.float32)
        lo, hi = (r0 - 5) * 64, (r0 + 21) * 64
        slo, shi = max(lo, 0), min(hi, SEQ)
        xkv[slo - lo: shi - lo] = x[slo:shi]
        xkv[C.TKW:] = ctx
        m["x_kv"] = xkv
        qlo = (r0 - 1) * 64
        cq = np.ones((64, C.TQ), np.float32)
        sq = np.zeros((64, C.TQ), np.float32)
        msk = np.zeros((128, C.TQ), np.float32)
        a, b = max(qlo, 0), min(qlo + C.TQ, SEQ)
        cq[:, a - qlo: b - qlo] = cos2[:, a:b]
        sq[:, a - qlo: b - qlo] = sin2[:, a:b]
        msk[:, a - qlo: b - qlo] = 1.0
        m["cosq"], m["sinq"], m["cmask"] = cq, sq, msk
        n = np.arange(384)
        p = np.arange(128)
        tab = np.empty((3, C.NAH, 7, 128, 384), np.float32)
        for b_ in range(3):
            qr = (r0 - 1 + 6 * b_ + n // 64)[None, :]
            qc = (n % 64)[None, :]
            for c_ in range(7):
                kr = (r0 - 5 + 6 * b_ + 2 * c_ + p // 64)[:, None]
                kc = (p % 64)[:, None]
                rs = np.clip(qr - 4, 0, C.ROWS - 8)
                cs = np.clip(qc - 8, 0, 64 - 16)
                valid = (kr >= 0) & (kr < C.ROWS) & (qr >= 0) & (qr < C.ROWS) & (kr >= rs) & (kr < rs + 8) \
                    & (kc >= cs) & (kc < cs + 16)
                ri = np.clip(kr - qr + 7, 0, 14)
                cidx = np.clip(kc - qc, -15, 15) + 15
                g = rpb[:, ri, cidx]
                tab[b_, :, c_] = np.where(valid[None], g, np.float32(-30000.0))
        m["na_bias"] = tab.reshape(3 * C.NAH * 7 * 128, 384)
        maps.append(m)
    return maps


def build(C, dbg=(), stop_after=None):
    nc = bass.Bass("TRN2", target_bir_lowering=False)
    D, DC, TQ, TK, TA = C.D, C.DC, C.TQ, C.TK, C.TA
    with ExitStack() as stack:
        S = Sched(nc, stack)
        I = {}

        def inp(name, shape):
            I[name] = Buf(name, nc.dram_tensor(name, list(shape), F32, kind="ExternalInput").ap())

        for name, shape in [
            ("x_allT", [D, C.SEQ]), ("ctxT", [D, C.CTX]), ("g_mix_c", [128, C.L * DC]), ("x_kv", [TK, D]), ("c2_l", [128, DC * 2]),
            ("ada_w", [C.L * D, 6 * D]), ("ada_b", [C.L, 6 * D]), ("g_mix", [C.L, D]), ("g_mlp", [C.L, D]),
            ("mlp_w1", [C.L * D, C.DFF]), ("mlp_w2", [C.L * C.DFF, D]), ("w_in", [D, C.INW]),
            ("g_qa_l", [128, C.QC]), ("w_qb", [C.QR, C.MLAH * 192]), ("g_kva_l", [128, C.KC]),
            ("w_kvb", [C.KVR, C.MLAH * 256]), ("g_na_q", [128, 1]), ("g_na_k", [128, 1]),
            ("g_mq_n", [128, 1]), ("g_mq_r", [64, 1]), ("g_mk_n", [128, 1]), ("g_mk_r", [64, 1]),
            ("w_out", [C.NAW + C.MLAW, D]), ("w_pw1", [D, 2 * D]), ("b_pw1_l", [128, 2 * DC]),
            ("w_dw_l", [128, DC, C.CONVW]), ("b_dw_l", [128, DC]), ("g_ln_l", [128, DC]), ("b_ln_l", [128, DC]),
            ("w_pw2", [D, D]), ("b_pw2", [1, D]), ("cosk", [64, TA]), ("sink", [64, TA]), ("rmat", [64, 64]),
            ("ident", [128, 128]), ("sel", [4, 128]), ("cosq", [64, TQ]), ("sinq", [64, TQ]), ("cmask", [128, TQ]),
            ("na_bias", [3 * C.NAH * 7 * 128, 384]),
        ]:
            inp(name, shape)
        out = Buf("out", nc.dram_tensor("out", [C.RPC * 64, D], F32, kind="ExternalOutput").ap())
        out.is_dram_out = True

        def scr(name, shape, dt):
            return S.dram(name, shape, dt, kind=("ExternalOutput" if name in dbg else "Internal"))

        modrow = scr("modrow", [C.L * 2, 6 * D], F32)
        kvnT = scr("kvnT", [128, C.KC, TA], BF16)
        kr0 = scr("kr0", [64, TA], F32)
        sqpe = scr("sqpe", [64, TA], BF16)
        hTkv = scr("hTkv", [128, DC, TK], BF16)
        naqT = scr("naqT", [C.NAH, 128, TQ], BF16)
        nakT = scr("nakT", [C.NAH, 128, TK], BF16)
        nav = scr("nav", [C.NAH, 128, TK // 128, 128], BF16)
        qlraw = scr("qlraw", [128, C.QC, TQ], F32)
        mqT = scr("mqT", [C.MLAH, 192, TQ], BF16)
        mkT = scr("mkT", [C.MLAH, 192, TA], BF16)
        mv = scr("mv", [C.MLAH, 128, TA // 128, 128], BF16)
        attnT = scr("attnT", [128, DC, TQ], BF16)
        xres = scr("xres", [TQ, D], F32)
        yacc = scr("yacc", [TQ, D], F32)
        hT = scr("hT", [128, DC, TQ], BF16)
        uT = scr("uT", [128, DC, TQ], F32)

        identb = S.sbuf("identb", [128, 128], BF16)
        onesb = S.sbuf("onesb", [128, 128], BF16)
        rmat = S.sbuf("rmat", [64, 64], F32)
        self_ = S.sbuf("sel", [2, 2, 128], F32)
        S.dma("pool", identb[:, :], I["ident"][:, :], reads=[I["ident"]], writes=[identb])
        onesf_p = S.sbuf("onesf_p", [128, 128], F32)
        S.op("pool", lambda e: e.memset(onesf_p[:, :], 1.0), writes=[onesf_p])
        identf = S.sbuf("identf", [128, 128], F32)
        S.dma("sp", identf[:, :], I["ident"][:, :], reads=[I["ident"]], writes=[identf])
        S.dma("sp", rmat[:, :], I["rmat"][:, :], reads=[I["rmat"]], writes=[rmat])
        S.dma("sp", self_[:, :, :], I["sel"].ap.rearrange("(a b) m -> a b m", a=2), reads=[I["sel"]], writes=[self_])
        S.op("pool", lambda e: e.memset(onesb[:, :], 1.0), writes=[onesb])
        PS = [S.psum(f"ps{i}", [128, 512], F32) for i in range(8)]

        def psb(i):
            return PS[i].ap[:, :].bitcast(BF16)

        def flush():
            S.barrier()
            S.replay()

        def bcast_row(dst, row_ap, rd):
            S.dma("sp", dst[:, :], row_ap.partition_broadcast(128), reads=[rd], writes=[dst])

        def rstd_from(dst_ap, src_ap, n, eps, reads, writes):
            S.op("act", lambda e: e.activation(out=dst_ap, in_=src_ap, func=AF.Sqrt, bias=float(eps), scale=1.0 / n),
                 reads=reads, writes=writes)
            S.op("dve", lambda e: e.reciprocal(out=dst_ap, in_=dst_ap), reads=writes, writes=writes)

        scb = S.sbuf("scb", [128, DC, 2], BF16)
        with ExitStack() as st:
            cl = S.sbuf("cl", [128, DC, 2], F32, st)
            S.dma("sp", cl[:, :, :], I["c2_l"].ap.rearrange("p (k t) -> p k t", t=2), reads=[I["c2_l"]], writes=[cl])
            S.op("act", lambda e: e.activation(out=scb[:, :, :], in_=cl[:, :, :], func=AF.Silu), reads=[cl], writes=[scb])
            CBW = min(4096, 6 * D)
            NBK = CBW // 512
            wt = [S.sbuf(f"adaw{i}", [128, CBW], BF16, st) for i in range(4)]
            bt = S.sbuf("adab", [2, CBW], F32, st)
            rt = S.sbuf("adar", [2, CBW], F32, st)
            it = 0
            for l in range(1):
                for cb in range(6 * D // CBW):
                    cs = slice(cb * CBW, (cb + 1) * CBW)
                    S.dma("sp", bt[:, :], I["ada_b"].ap[l:l + 1, cs].partition_broadcast(2), reads=[I["ada_b"]], writes=[bt])
                    for k in range(DC):
                        w = wt[it % 4]
                        it += 1
                        S.dma("pool", w[:, :], I["ada_w"].ap[l * D + k * 128:l * D + (k + 1) * 128, cs], reads=[I["ada_w"]], writes=[w])
                        for n_ in range(NBK):
                            S.op("pe", lambda e, w=w, k=k, n_=n_: e.matmul(PS[n_][0:2, :], lhsT=scb[:, k, :], rhs=w[:, n_ * 512:(n_ + 1) * 512],
                                                                          start=(k == 0), stop=(k == DC - 1)),
                                 reads=[scb, w], writes=[PS[n_]], inc=(n_ == NBK - 1))
                    for n_ in range(NBK):
                        ns = slice(n_ * 512, (n_ + 1) * 512)
                        S.op("dve", lambda e, n_=n_, ns=ns: e.tensor_tensor(out=rt[:, ns], in0=PS[n_][0:2, :], in1=bt[:, ns], op=ALU.add),
                             reads=[PS[n_], bt], writes=[(rt, n_)])
                    S.dma("sp", modrow.ap[2 * l:2 * l + 2, cs], rt[:, :], reads=[rt], writes=[(modrow, (l, cb))])
            flush()
        if stop_after == "S0":
            return nc

        def mmgroup_kt(ps_aps, lhs_fn, rhs_fn, nk, reads, writes):
            nt_ = len(ps_aps)
            for k in range(nk):
                for t in range(nt_):
                    S.op("pe", lambda e, k=k, t=t: e.matmul(ps_aps[t], lhsT=lhs_fn(k), rhs=rhs_fn(k, t), start=(k == 0), stop=(k == nk - 1)),
                         reads=reads, writes=[writes[t]], inc=(k == nk - 1))

        def mmgroup(ps_ap, pairs, reads, writes):
            n = len(pairs)
            for i, (l, r) in enumerate(pairs):
                S.op("pe", lambda e, l=l, r=r, i=i: e.matmul(ps_ap, lhsT=l, rhs=r, start=(i == 0), stop=(i == n - 1)),
                     reads=reads, writes=writes, inc=(i == n - 1))

        def norm_stage(st, tiles, gvec_ap, gbuf, l, j_shift, j_scale, variant, consume, blk=4):
            Gt = S.sbuf("Gt", [128, D], F32, st)
            St = S.sbuf("St", [128, D], F32, st)
            row = 2 * l + variant
            bcast_row(Gt, modrow.ap[row:row + 1, j_scale * D:(j_scale + 1) * D], modrow)
            bcast_row(St, gvec_ap, gbuf)
            S.op("dve", lambda e: e.scalar_tensor_tensor(out=Gt[:, :], in0=Gt[:, :], scalar=1.0, in1=St[:, :],
                                                         op0=ALU.add, op1=ALU.mult), reads=[Gt, St], writes=[Gt])
            bcast_row(St, modrow.ap[row:row + 1, j_shift * D:(j_shift + 1) * D], modrow)
            xt = [S.sbuf(f"xt{i}", [128, D], F32, st) for i in range(2)]
            hb = [S.sbuf(f"hb{i}", [128, D], BF16, st) for i in range(2)]
            ss = [S.sbuf(f"ss{i}", [128, 1], F32, st) for i in range(2)]
            rs = [S.sbuf(f"rs{i}", [128, 1], F32, st) for i in range(2)]
            hblk = [S.sbuf(f"hblk{i}", [128, DC, blk * 128], BF16, st) for i in range(2)]
            G8 = min(8, DC)
            tcount = 0
            for b0 in range(0, len(tiles), blk):
                grp = tiles[b0:b0 + blk]
                hk = hblk[(b0 // blk) % 2]
                for ti, (src, r0) in enumerate(grp):
                    b = tcount % 2
                    tcount += 1
                    S.dma("sp", xt[b][:, :], src.ap[r0:r0 + 128, :], reads=[src], writes=[xt[b]])
                    S.op("pool", lambda e, b=b: e.memset(ss[b][:, :], 0.0), writes=[ss[b]])
                    S.op("act", lambda e, b=b: e.activation(out=hb[b][:, :], in_=xt[b][:, :], func=AF.Square, accum_out=ss[b][:, :]),
                         reads=[xt[b], ss[b]], writes=[hb[b], ss[b]])
                    rstd_from(rs[b][:, :], ss[b][:, :], D, C.EPS, [ss[b]], [rs[b]])
                    S.op("dve", lambda e, b=b: e.scalar_tensor_tensor(out=xt[b][:, :], in0=xt[b][:, :], scalar=rs[b][:, 0:1], in1=Gt[:, :],
                                                                      op0=ALU.mult, op1=ALU.mult), reads=[xt[b], rs[b], Gt], writes=[xt[b]])
                    S.op("dve", lambda e, b=b: e.tensor_tensor(out=hb[b][:, :], in0=xt[b][:, :], in1=St[:, :], op=ALU.add),
                         reads=[xt[b], St], writes=[hb[b]])
                    for k0 in range(0, DC, G8):
                        bank = 6 + ((k0 // G8) % 2)
                        pv = psb(bank)
                        for kk in range(G8):
                            k = k0 + kk
                            S.op("pe", lambda e, pv=pv, kk=kk, k=k, b=b: e.transpose(out=pv[:, kk * 128:(kk + 1) * 128],
                                                                                      in_=hb[b][:, k * 128:(k + 1) * 128], identity=identb[:, :]),
                                 reads=[hb[b], identb], writes=[PS[bank]], inc=(kk == G8 - 1))
                        eng = "act" if (k0 // G8) % 2 == 0 else "dve"
                        src_v = pv[:, 0:G8 * 128].rearrange("p (k t) -> p k t", k=G8)
                        dst_v = hk[:, k0:k0 + G8, ti * 128:(ti + 1) * 128]
                        if eng == "act":
                            S.op("act", lambda e, s=src_v, d=dst_v: e.copy(out=d, in_=s), reads=[PS[bank]], writes=[(hk, ti)])
                        else:
                            S.op("dve", lambda e, s=src_v, d=dst_v: e.tensor_copy(out=d, in_=s), reads=[PS[bank]], writes=[(hk, ti)])
                consume(hk, b0 * 128, len(grp) * 128)


        def colvec(st, dst, row, j):
            tmp = S.sbuf("cv_tmp", [DC, 128], F32, st)
            S.dma("sp", tmp[:, :], modrow.ap[row:row + 1, j * D:(j + 1) * D].rearrange("o (k p) -> (o k) p", p=128), reads=[modrow], writes=[tmp])
            S.op("pe", lambda e: e.matmul(PS[7][:, 0:DC], lhsT=tmp[:, :], rhs=identf[0:DC, 0:DC], start=True, stop=True),
                 reads=[tmp, identf], writes=[PS[7]])
            S.op("dve", lambda e: e.tensor_copy(out=dst[:, :], in_=PS[7][:, 0:DC]), reads=[PS[7]], writes=[dst])

        def norm_fm_stage(st, xT, ntot, l, variant, consume, blk_tok=512):
            row = 2 * l + variant
            Gc = S.sbuf("Gc", [128, DC], F32, st)
            Sc = S.sbuf("Sc", [128, DC], F32, st)
            gc = S.sbuf("gc", [128, DC], F32, st)
            S.dma("sp", gc[:, :], I["g_mix_c"].ap[:, l * DC:(l + 1) * DC], reads=[I["g_mix_c"]], writes=[gc])
            colvec(st, Gc, row, 1)
            colvec(st, Sc, row, 0)
            S.op("dve", lambda e: e.scalar_tensor_tensor(out=Gc[:, :], in0=Gc[:, :], scalar=1.0, in1=gc[:, :], op0=ALU.add, op1=ALU.mult),
                 reads=[Gc, gc], writes=[Gc])
            xk = [S.sbuf(f"xk{i}", [128, blk_tok], F32, st) for i in range(4)]
            sqb = [S.sbuf(f"sqb{i}", [128, blk_tok], F32, st) for i in range(4)]
            sacc = [S.sbuf(f"sacc{i}", [128, blk_tok], F32, st) for i in range(4)]
            tmp = [S.sbuf(f"ntmp{i}", [128, blk_tok], F32, st) for i in range(2)]
            rsd = [S.sbuf(f"nrs{i}", [128, blk_tok], F32, st) for i in range(2)]
            hblk = [S.sbuf(f"hblkf{i}", [128, DC, blk_tok], BF16, st) for i in range(2)]
            cx = 0
            units = []
            nstep = 2 * DC

            def pump(step):
                if units and step % max(1, nstep // max(1, pump.total)) == 0:
                    units.pop(0)()
            pump.total = 1
            for bi, a0 in enumerate(range(0, ntot, blk_tok)):
                ntok = min(blk_tok, ntot - a0)
                n = slice(0, ntok)
                ssp = PS[5 + bi % 2]
                hk = hblk[bi % 2]
                r_ = rsd[bi % 2]
                aE, aO = sacc[2 * (bi % 2)], sacc[2 * (bi % 2) + 1]
                step = 0
                for k in range(DC):
                    x_ = xk[cx % 4]
                    q_ = sqb[cx % 4]
                    cx += 1
                    S.dma("sp", x_[:, n], xT.ap[k * 128:(k + 1) * 128, a0:a0 + ntok], reads=[xT], writes=[x_])
                    S.op("act", lambda e, x_=x_, q_=q_: e.activation(out=q_[:, n], in_=x_[:, n], func=AF.Square), reads=[x_], writes=[q_])
                    a_ = aE if k % 2 == 0 else aO
                    if k < 2:
                        S.op("dve", lambda e, a_=a_, q_=q_: e.tensor_copy(out=a_[:, n], in_=q_[:, n]), reads=[q_], writes=[a_])
                    else:
                        S.op("dve", lambda e, a_=a_, q_=q_: e.tensor_tensor(out=a_[:, n], in0=a_[:, n], in1=q_[:, n], op=ALU.add), reads=[a_, q_], writes=[a_])
                    pump(step)
                    step += 1
                S.op("pe", lambda e, aE=aE, ssp=ssp: e.matmul(ssp[:, n], lhsT=onesf_p[:, :], rhs=aE[:, n], start=True, stop=(DC < 2)),
                     reads=[onesf_p, aE], writes=[ssp], inc=(DC < 2))
                if DC >= 2:
                    S.op("pe", lambda e, aO=aO, ssp=ssp: e.matmul(ssp[:, n], lhsT=onesf_p[:, :], rhs=aO[:, n], start=False, stop=True),
                         reads=[onesf_p, aO], writes=[ssp], inc=True)
                rstd_from(r_[:, n], ssp[:, n], D, C.EPS, [ssp], [r_])
                for k in range(DC):
                    x_ = xk[cx % 4]
                    t_ = tmp[cx % 2]
                    cx += 1
                    S.dma("sp", x_[:, n], xT.ap[k * 128:(k + 1) * 128, a0:a0 + ntok], reads=[xT], writes=[x_])
                    S.op("dve", lambda e, x_=x_, t_=t_, r_=r_: e.tensor_tensor(out=t_[:, n], in0=x_[:, n], in1=r_[:, n], op=ALU.mult), reads=[x_, r_], writes=[t_])
                    S.op("act", lambda e, t_=t_, hk=hk, k=k: e.activation(out=hk[:, k, n], in_=t_[:, n], func=AF.Identity, bias=Sc[:, k:k + 1], scale=Gc[:, k:k + 1]),
                         reads=[t_, Sc, Gc], writes=[(hk, k)])
                    pump(step)
                    step += 1
                while units:
                    units.pop(0)()
                res_ = consume(hk, a0, ntok)
                units = list(res_) if res_ else []
                pump.total = max(1, len(units))
            while units:
                units.pop(0)()

        def store_hT(dst, col0):
            def f(hk, toff, ntok):
                a0 = col0 + toff
                S.dma("sp", dst.ap[:, :, a0:a0 + ntok], hk[:, :, 0:ntok], reads=[hk], writes=[(dst, ("c", a0))])
            return f

        KVR, KC = C.KVR, C.KC
        with ExitStack() as st:
            wkvl = S.sbuf("wkvl", [128, DC, KVR + 64], BF16, st)
            S.dma("pool", wkvl[:, :, :], I["w_in"].ap[:, C.OFF_KV_LAT:C.INW].rearrange("(k p) n -> p k n", p=128),
                  reads=[I["w_in"]], writes=[wkvl])
            gkva = S.sbuf("gkva", [128, KC], F32, st)
            gmkr = S.sbuf("gmkr", [64, 1], F32, st)
            S.dma("sp", gkva[:, :], I["g_kva_l"][:, :], reads=[I["g_kva_l"]], writes=[gkva])
            S.dma("sp", gmkr[:, :], I["g_mk_r"][:, :], reads=[I["g_mk_r"]], writes=[gmkr])
            raw = S.sbuf("raw", [128, KC, 512], F32, st)
            sq = [S.sbuf(f"sq{i}", [128, 512], BF16, st) for i in range(2)]
            kpe = S.sbuf("kpe", [64, 512], F32, st)
            sqp = S.sbuf("sqp", [64, 512], BF16, st)
            rstd = S.sbuf("rstd", [128, 512], F32, st)
            kvn_t = S.sbuf("kvn_t", [128, KC, 512], BF16, st)
            xk = S.sbuf("xk", [64, 512], F32, st)
            cst = S.sbuf("cst", [64, 512], F32, st)
            snt = S.sbuf("snt", [64, 512], F32, st)
            t1 = S.sbuf("t1", [64, 512], F32, st)
            krt = S.sbuf("krt", [64, 512], F32, st)

            def s1_consume(base):
                def f(hk, toff, ntok):
                    a0 = base + toff
                    n = slice(0, ntok)
                    us = []

                    def u_loads():
                        S.dma("sp", cst[:, n], I["cosk"].ap[:, a0:a0 + ntok], reads=[I["cosk"]], writes=[cst])
                        S.dma("sp", snt[:, n], I["sink"].ap[:, a0:a0 + ntok], reads=[I["sink"]], writes=[snt])
                    us.append(u_loads)
                    for m in range(KC):
                        def u_m(m=m):
                            p = PS[m % 2]
                            mmgroup(p[:, n], [(wkvl[:, k, m * 128:(m + 1) * 128], hk[:, k, n]) for k in range(DC)], [wkvl, hk], [p])
                            S.op("act", lambda e, p=p, m=m: e.copy(out=raw[:, m, n], in_=p[:, n]), reads=[p], writes=[(raw, m)])
                            s_ = sq[m % 2]
                            S.op("act", lambda e, p=p, s_=s_: e.activation(out=s_[:, n], in_=p[:, n], func=AF.Square), reads=[p], writes=[s_])
                            S.op("pe", lambda e, s_=s_, m=m: e.matmul(PS[2][:, n], lhsT=onesb[:, :], rhs=s_[:, n], start=(m == 0), stop=(m == KC - 1)),
                                 reads=[s_, onesb], writes=[PS[2]], inc=True)
                        us.append(u_m)

                    def u_rope_mm():
                        p = PS[3]
                        mmgroup(p[0:64, n], [(wkvl[:, k, KVR:KVR + 64], hk[:, k, n]) for k in range(DC)], [wkvl, hk], [p])
                        S.op("act", lambda e: e.copy(out=kpe[:, n], in_=p[0:64, n]), reads=[p], writes=[kpe])
                        S.op("act", lambda e: e.activation(out=sqp[:, n], in_=p[0:64, n], func=AF.Square), reads=[p], writes=[sqp])
                        S.dma("sp", sqpe.ap[:, a0:a0 + ntok], sqp[:, n], reads=[sqp], writes=[(sqpe, a0)])
                    us.append(u_rope_mm)

                    def u_norm():
                        rstd_from(rstd[:, n], PS[2][:, n], KVR, C.EPS, [PS[2]], [rstd])
                        for m in range(KC):
                            S.op("dve", lambda e, m=m: e.scalar_tensor_tensor(out=kvn_t[:, m, n], in0=raw[:, m, n], scalar=gkva[:, m:m + 1],
                                                                              in1=rstd[:, n], op0=ALU.mult, op1=ALU.mult),
                                 reads=[(raw, m), gkva, rstd], writes=[(kvn_t, m)])
                        S.dma("sp", kvnT.ap[:, :, a0:a0 + ntok], kvn_t[:, :, n], reads=[kvn_t], writes=[(kvnT, a0)])
                    us.append(u_norm)

                    def u_rope():
                        S.op("dve", lambda e: e.tensor_scalar(out=xk[:, n], in0=kpe[:, n], scalar1=gmkr[:, 0:1], scalar2=None, op0=ALU.mult),
                             reads=[kpe, gmkr], writes=[xk])
                        S.op("pe", lambda e: e.matmul(PS[4][0:64, n], lhsT=rmat[:, :], rhs=xk[:, n], start=True, stop=True),
                             reads=[rmat, xk], writes=[PS[4]])
                        S.op("dve", lambda e: e.tensor_tensor(out=t1[:, n], in0=xk[:, n], in1=cst[:, n], op=ALU.mult), reads=[xk, cst], writes=[t1])
                        S.op("dve", lambda e: e.tensor_tensor(out=krt[:, n], in0=PS[4][0:64, n], in1=snt[:, n], op=ALU.mult), reads=[PS[4], snt], writes=[krt])
                        S.op("dve", lambda e: e.tensor_tensor(out=krt[:, n], in0=krt[:, n], in1=t1[:, n], op=ALU.add), reads=[krt, t1], writes=[krt])
                        S.dma("sp", kr0.ap[:, a0:a0 + ntok], krt[:, n], reads=[krt], writes=[(kr0, a0)])
                    us.append(u_rope)
                    return us
                return f

            with ExitStack() as st2:
                norm_fm_stage(st2, I["x_allT"], C.SEQ, 0, 0, s1_consume(0))
                flush()
            with ExitStack() as st2:
                norm_fm_stage(st2, I["ctxT"], C.CTX, 0, 1, s1_consume(C.SEQ))
                flush()
        with ExitStack() as st2:
            norm_stage(st2, [(I["x_kv"], i * 128) for i in range(C.TKW // 128)], I["g_mix"].ap[0:1, :], I["g_mix"], 0, 0, 1, 0, store_hT(hTkv, 0))
            flush()
        with ExitStack() as st2:
            norm_stage(st2, [(I["x_kv"], C.TKW + i * 128) for i in range(C.CTX // 128)], I["g_mix"].ap[0:1, :], I["g_mix"], 0, 0, 1, 1, store_hT(hTkv, C.TKW))
            flush()
        if stop_after == "S1":
            return nc

        NT = C.NT
        NQT = TQ // NT
        NKT = 4
        KT = TK // NKT
        assert KT <= 512 and KT * NKT == TK
        TKT = TK // 128
        qlrstd = scr("qlrstd", [128, TQ], F32)
        with ExitStack() as st:
            hk = S.sbuf("hTkv_sb", [128, DC, TK], BF16, st)
            for k in range(DC):
                S.dma("sp", hk[:, k, :], hTkv.ap[:, k, :], reads=[hTkv], writes=[(hk, k)])
            NB = 2
            wt = [S.sbuf(f"winw{i}", [128, DC, 256], BF16, st) for i in range(NB)]
            gq = S.sbuf("gnaq", [128, 1], F32, st)
            gk = S.sbuf("gnak", [128, 1], F32, st)
            S.dma("sp", gq[:, :], I["g_na_q"][:, :], reads=[I["g_na_q"]], writes=[gq])
            S.dma("sp", gk[:, :], I["g_na_k"][:, :], reads=[I["g_na_k"]], writes=[gk])
            rawt = [S.sbuf(f"rawt{i}", [128, 512], F32, st) for i in range(2)]
            sqt = [S.sbuf(f"sqt{i}", [128, 512], BF16, st) for i in range(2)]
            rst = [S.sbuf(f"rst{i}", [128, 512], F32, st) for i in range(2)]
            outt = [S.sbuf(f"outt{i}", [128, 512], BF16, st) for i in range(2)]
            vsb = [S.sbuf(f"vsb{i}", [128, TKT, 256], BF16, st) for i in range(2)]
            cols = list(range(0, C.OFF_KV_LAT, 256))
            cnt = 0
            for bi, c0 in enumerate(cols):
                w = wt[bi % NB]
                S.dma("pool", w[:, :, :], I["w_in"].ap[:, c0:c0 + 256].rearrange("(k p) n -> p k n", p=128),
                      reads=[I["w_in"]], writes=[w])
                if 2 * C.NAW <= c0 < 3 * C.NAW:
                    vb = vsb[bi % 2]
                    for tt in range(TKT):
                        p = PS[tt % 2]
                        mmgroup(p[:, 0:256], [(hk[:, k, tt * 128:(tt + 1) * 128], w[:, k, :]) for k in range(DC)], [hk, w], [p])
                        if tt % 2 == 0:
                            S.op("act", lambda e, p=p, tt=tt, vb=vb: e.copy(out=vb[:, tt, :], in_=p[:, 0:256]), reads=[p], writes=[(vb, tt)])
                        else:
                            S.op("dve", lambda e, p=p, tt=tt, vb=vb: e.tensor_copy(out=vb[:, tt, :], in_=p[:, 0:256]), reads=[p], writes=[(vb, tt)])
                    for sub in range(2):
                        h = (c0 - 2 * C.NAW) // 128 + sub
                        S.dma("sp", nav.ap[h], vb[:, :, sub * 128:(sub + 1) * 128], reads=[vb], writes=[(nav, h)])
                    continue
                for sub in range(2):
                    mc = c0 + sub * 128
                    wsl = lambda k, w=w, sub=sub: w[:, k, sub * 128:(sub + 1) * 128]
                    if mc < 2 * C.NAW:
                        isq = mc < C.NAW
                        h = (mc if isq else mc - C.NAW) // 128
                        ntile, nsz, off = (NQT, NT, C.QOFF) if isq else (NKT, KT, 0)
                        gvec = gq if isq else gk
                        dst = naqT if isq else nakT
                        for t in range(ntile):
                            cnt += 1
                            b = cnt % 2
                            p = PS[b]
                            n = slice(0, nsz)
                            a0 = off + t * nsz
                            mmgroup(p[:, n], [(wsl(k), hk[:, k, a0:a0 + nsz]) for k in range(DC)], [hk, w], [p])
                            S.op("act", lambda e, p=p, b=b, n=n: e.copy(out=rawt[b][:, n], in_=p[:, n]), reads=[p], writes=[rawt[b]])
                            S.op("act", lambda e, p=p, b=b, n=n: e.activation(out=sqt[b][:, n], in_=p[:, n], func=AF.Square), reads=[p], writes=[sqt[b]])
                            q = PS[2 + b]
                            S.op("pe", lambda e, q=q, b=b, n=n: e.matmul(q[:, n], lhsT=onesb[:, :], rhs=sqt[b][:, n], start=True, stop=True),
                                 reads=[sqt[b], onesb], writes=[q])
                            rstd_from(rst[b][:, n], q[:, n], 128, C.EPS, [q], [rst[b]])
                            S.op("dve", lambda e, b=b, n=n, gvec=gvec: e.scalar_tensor_tensor(out=outt[b][:, n], in0=rawt[b][:, n], scalar=gvec[:, 0:1],
                                                                                             in1=rst[b][:, n], op0=ALU.mult, op1=ALU.mult),
                                 reads=[rawt[b], gvec, rst[b]], writes=[outt[b]])
                            S.dma("sp", dst.ap[h, :, t * nsz:(t + 1) * nsz], outt[b][:, n], reads=[outt[b]], writes=[(dst, (h, t))])
                    else:
                        j = (mc - C.OFF_Q_LAT) // 128
                        for t in range(NQT):
                            cnt += 1
                            b = cnt % 2
                            p = PS[b]
                            n = slice(0, NT)
                            a0 = C.QOFF + t * NT
                            mmgroup(p[:, n], [(wsl(k), hk[:, k, a0:a0 + NT]) for k in range(DC)], [hk, w], [p])
                            S.op("act", lambda e, p=p, b=b, n=n: e.copy(out=rawt[b][:, n], in_=p[:, n]), reads=[p], writes=[rawt[b]])
                            S.op("act", lambda e, p=p, b=b, n=n: e.activation(out=sqt[b][:, n], in_=p[:, n], func=AF.Square), reads=[p], writes=[sqt[b]])
                            q = PS[5 + t]
                            S.op("pe", lambda e, q=q, b=b, n=n, j=j: e.matmul(q[:, n], lhsT=onesb[:, :], rhs=sqt[b][:, n], start=(j == 0), stop=(j == C.QC - 1)),
                                 reads=[sqt[b], onesb], writes=[q])
                            S.dma("sp", qlraw.ap[:, j, t * NT:(t + 1) * NT], rawt[b][:, n], reads=[rawt[b]], writes=[(qlraw, (j, t))])
            for t in range(NQT):
                rstd_from(rst[t % 2][:, 0:NT], PS[5 + t][:, 0:NT], C.QR, C.EPS, [PS[5 + t]], [rst[t % 2]])
                S.dma("sp", qlrstd.ap[:, t * NT:(t + 1) * NT], rst[t % 2][:, 0:NT], reads=[rst[t % 2]], writes=[(qlrstd, t)])
            flush()
        if stop_after == "S2":
            return nc

        QC = C.QC
        with ExitStack() as st:
            qn = S.sbuf("qn", [128, QC, TQ], BF16, st)
            gqa = S.sbuf("gqa", [128, QC], F32, st)
            gn = S.sbuf("gmqn", [128, 1], F32, st)
            gr = S.sbuf("gmqr", [64, 1], F32, st)
            S.dma("sp", gqa[:, :], I["g_qa_l"][:, :], reads=[I["g_qa_l"]], writes=[gqa])
            S.dma("sp", gn[:, :], I["g_mq_n"][:, :], reads=[I["g_mq_n"]], writes=[gn])
            S.dma("sp", gr[:, :], I["g_mq_r"][:, :], reads=[I["g_mq_r"]], writes=[gr])
            cq = S.sbuf("cosq", [64, TQ], F32, st)
            sq_ = S.sbuf("sinq", [64, TQ], F32, st)
            S.dma("sp", cq[:, :], I["cosq"][:, :], reads=[I["cosq"]], writes=[cq])
            S.dma("sp", sq_[:, :], I["sinq"][:, :], reads=[I["sinq"]], writes=[sq_])
            with ExitStack() as st2:
                qrs = S.sbuf("qrs", [128, TQ], F32, st2)
                S.dma("sp", qrs[:, :], qlrstd.ap[:, :], reads=[qlrstd], writes=[qrs])
                rl = [S.sbuf(f"qlr{i}", [128, TQ], F32, st2) for i in range(2)]
                for j in range(QC):
                    b = j % 2
                    S.dma("sp", rl[b][:, :], qlraw.ap[:, j, :], reads=[qlraw], writes=[rl[b]])
                    S.op("dve", lambda e, b=b, j=j: e.scalar_tensor_tensor(out=qn[:, j, :], in0=rl[b][:, :], scalar=gqa[:, j:j + 1], in1=qrs[:, :],
                                                                          op0=ALU.mult, op1=ALU.mult), reads=[rl[b], gqa, qrs], writes=[(qn, j)])
                flush()
            wq = [S.sbuf(f"wq{i}", [128, QC, 192], BF16, st) for i in range(2)]
            rawn = [S.sbuf(f"rawn{i}", [128, NT], F32, st) for i in range(2)]
            rawr = [S.sbuf(f"rawr{i}", [64, NT], F32, st) for i in range(2)]
            sqn = [S.sbuf(f"sqn{i}", [128, NT], BF16, st) for i in range(2)]
            sqr = [S.sbuf(f"sqr{i}", [64, NT], BF16, st) for i in range(2)]
            rsq = [S.sbuf(f"rsq{i}", [128, NT], F32, st) for i in range(2)]
            on = [S.sbuf(f"on{i}", [128, NT], BF16, st) for i in range(2)]
            xr = [S.sbuf(f"xr{i}", [64, NT], F32, st) for i in range(2)]
            t1q = [S.sbuf(f"t1q{i}", [64, NT], F32, st) for i in range(2)]
            t2q = [S.sbuf(f"t2q{i}", [64, NT], F32, st) for i in range(2)]
            orr = [S.sbuf(f"orr{i}", [64, NT], BF16, st) for i in range(2)]
            cnt = 0
            for h in range(C.MLAH):
                w = wq[h % 2]
                S.dma("pool", w[:, :, :], I["w_qb"].ap[:, h * 192:(h + 1) * 192].rearrange("(k p) n -> p k n", p=128),
                      reads=[I["w_qb"]], writes=[w])
                for t in range(NQT):
                    cnt += 1
                    b = cnt % 2
                    ts_ = slice(t * NT, (t + 1) * NT)
                    pn, pr, pss, prot = PS[b], PS[2 + b], PS[4 + b], PS[6 + b]
                    mmgroup(pn[:, 0:NT], [(w[:, k, 0:128], qn[:, k, ts_]) for k in range(QC)], [w, qn], [pn])
                    mmgroup(pr[0:64, 0:NT], [(w[:, k, 128:192], qn[:, k, ts_]) for k in range(QC)], [w, qn], [pr])
                    S.op("act", lambda e, b=b, pn=pn: e.copy(out=rawn[b][:, :], in_=pn[:, 0:NT]), reads=[pn], writes=[rawn[b]])
                    S.op("act", lambda e, b=b, pn=pn: e.activation(out=sqn[b][:, :], in_=pn[:, 0:NT], func=AF.Square), reads=[pn], writes=[sqn[b]])
                    S.op("act", lambda e, b=b, pr=pr: e.copy(out=rawr[b][:, :], in_=pr[0:64, 0:NT]), reads=[pr], writes=[rawr[b]])
                    S.op("act", lambda e, b=b, pr=pr: e.activation(out=sqr[b][:, :], in_=pr[0:64, 0:NT], func=AF.Square), reads=[pr], writes=[sqr[b]])
                    S.op("pe", lambda e, b=b, pss=pss: e.matmul(pss[:, 0:NT], lhsT=onesb[:, :], rhs=sqn[b][:, :], start=True, stop=False),
                         reads=[sqn[b], onesb], writes=[pss], inc=False)
                    S.op("pe", lambda e, b=b, pss=pss: e.matmul(pss[:, 0:NT], lhsT=onesb[0:64, :], rhs=sqr[b][:, :], start=False, stop=True),
                         reads=[sqr[b], onesb], writes=[pss])
                    rstd_from(rsq[b][:, :], pss[:, 0:NT], 192, C.EPS, [pss], [rsq[b]])
                    S.op("dve", lambda e, b=b: e.scalar_tensor_tensor(out=on[b][:, :], in0=rawn[b][:, :], scalar=gn[:, 0:1], in1=rsq[b][:, :],
                                                                      op0=ALU.mult, op1=ALU.mult), reads=[rawn[b], gn, rsq[b]], writes=[on[b]])
                    S.dma("sp", mqT.ap[h, 0:128, ts_], on[b][:, :], reads=[on[b]], writes=[(mqT, (h, 0, t))])
                    S.op("dve", lambda e, b=b: e.scalar_tensor_tensor(out=xr[b][:, :], in0=rawr[b][:, :], scalar=gr[:, 0:1], in1=rsq[b][0:64, :],
                                                                      op0=ALU.mult, op1=ALU.mult), reads=[rawr[b], gr, rsq[b]], writes=[xr[b]])
                    S.op("pe", lambda e, b=b, prot=prot: e.matmul(prot[0:64, 0:NT], lhsT=rmat[:, :], rhs=xr[b][:, :], start=True, stop=True),
                         reads=[rmat, xr[b]], writes=[prot])
                    S.op("dve", lambda e, b=b, ts_=ts_: e.tensor_tensor(out=t1q[b][:, :], in0=xr[b][:, :], in1=cq[:, ts_], op=ALU.mult),
                         reads=[xr[b], cq], writes=[t1q[b]])
                    S.op("dve", lambda e, b=b, ts_=ts_, prot=prot: e.tensor_tensor(out=t2q[b][:, :], in0=prot[0:64, 0:NT], in1=sq_[:, ts_], op=ALU.mult),
                         reads=[prot, sq_], writes=[t2q[b]])
                    S.op("dve", lambda e, b=b: e.tensor_tensor(out=orr[b][:, :], in0=t1q[b][:, :], in1=t2q[b][:, :], op=ALU.add),
                         reads=[t1q[b], t2q[b]], writes=[orr[b]])
                    S.dma("sp", mqT.ap[h, 128:192, ts_], orr[b][:, :], reads=[orr[b]], writes=[(mqT, (h, 1, t))])
            flush()

        TAT = TA // 128
        with ExitStack() as st:
            kv = S.sbuf("kvn_sb", [128, KC, TA], BF16, st)
            for k in range(KC):
                S.dma("sp", kv[:, k, :], kvnT.ap[:, k, :], reads=[kvnT], writes=[(kv, k)])
            k0 = S.sbuf("kr0_sb", [64, TA], F32, st)
            sp_ = S.sbuf("sqpe_sb", [64, TA], BF16, st)
            S.dma("sp", k0[:, :], kr0.ap[:, :], reads=[kr0], writes=[k0])
            S.dma("sp", sp_[:, :], sqpe.ap[:, :], reads=[sqpe], writes=[sp_])
            gkn = S.sbuf("gmkn", [128, 1], F32, st)
            S.dma("sp", gkn[:, :], I["g_mk_n"][:, :], reads=[I["g_mk_n"]], writes=[gkn])
            wk = [S.sbuf(f"wk{i}", [128, KC, 256], BF16, st) for i in range(2)]
            rawk = [S.sbuf(f"rawk{i}", [128, 512], F32, st) for i in range(3)]
            sqk = [S.sbuf(f"sqk{i}", [128, 512], BF16, st) for i in range(3)]
            rsk = [S.sbuf(f"rsk{i}", [128, 512], F32, st) for i in range(3)]
            okn = [S.sbuf(f"okn{i}", [128, 512], BF16, st) for i in range(3)]
            okr = [S.sbuf(f"okr{i}", [64, 512], BF16, st) for i in range(3)]
            vh = [S.sbuf(f"vh{i}", [128, TAT, 128], BF16, st) for i in range(2)]
            cnt = 0
            for h in range(C.MLAH):
                w = wk[h % 2]
                S.dma("pool", w[:, :, :], I["w_kvb"].ap[:, h * 256:(h + 1) * 256].rearrange("(k p) n -> p k n", p=128),
                      reads=[I["w_kvb"]], writes=[w])
                pend = None
                for a0 in list(range(0, TA, 512)) + [None]:
                    if a0 is not None:
                        ntok = min(512, TA - a0)
                        n = slice(0, ntok)
                        cnt += 1
                        b = cnt % 3
                        p, pss = PS[b], PS[3 + b]
                        mmgroup(p[:, n], [(w[:, k, 0:128], kv[:, k, a0:a0 + ntok]) for k in range(KC)], [w, kv], [p])
                        S.op("act", lambda e, b=b, p=p, n=n: e.copy(out=rawk[b][:, n], in_=p[:, n]), reads=[p], writes=[rawk[b]])
                        S.op("act", lambda e, b=b, p=p, n=n: e.activation(out=sqk[b][:, n], in_=p[:, n], func=AF.Square), reads=[p], writes=[sqk[b]])
                    if pend is not None:
                        pb, pn_, pa0, pnt, ppss = pend
                        S.op("pe", lambda e, pb=pb, ppss=ppss, pn_=pn_: e.matmul(ppss[:, pn_], lhsT=onesb[:, :], rhs=sqk[pb][:, pn_], start=True, stop=False),
                             reads=[sqk[pb], onesb], writes=[ppss], inc=False)
                        S.op("pe", lambda e, ppss=ppss, pn_=pn_, pa0=pa0, pnt=pnt: e.matmul(ppss[:, pn_], lhsT=onesb[0:64, :], rhs=sp_[:, pa0:pa0 + pnt], start=False, stop=True),
                             reads=[sp_, onesb], writes=[ppss])
                        rstd_from(rsk[pb][:, pn_], ppss[:, pn_], 192, C.EPS, [ppss], [rsk[pb]])
                        S.op("dve", lambda e, pb=pb, pn_=pn_: e.scalar_tensor_tensor(out=okn[pb][:, pn_], in0=rawk[pb][:, pn_], scalar=gkn[:, 0:1], in1=rsk[pb][:, pn_],
                                                                                     op0=ALU.mult, op1=ALU.mult), reads=[rawk[pb], gkn, rsk[pb]], writes=[okn[pb]])
                        S.dma("sp", mkT.ap[h, 0:128, pa0:pa0 + pnt], okn[pb][:, pn_], reads=[okn[pb]], writes=[(mkT, (h, 0, pa0))])
                        S.op("dve", lambda e, pb=pb, pn_=pn_, pa0=pa0, pnt=pnt: e.tensor_tensor(out=okr[pb][:, pn_], in0=k0[:, pa0:pa0 + pnt], in1=rsk[pb][0:64, pn_], op=ALU.mult),
                             reads=[k0, rsk[pb]], writes=[okr[pb]])
                        S.dma("sp", mkT.ap[h, 128:192, pa0:pa0 + pnt], okr[pb][:, pn_], reads=[okr[pb]], writes=[(mkT, (h, 1, pa0))])
                    pend = (b, n, a0, ntok, pss) if a0 is not None else None
                vb = vh[h % 2]
                for t0 in range(0, TAT, 4):
                    nt_ = min(4, TAT - t0)
                    cnt += 1
                    p = PS[6 + cnt % 2]
                    for ti in range(nt_):
                        tt = t0 + ti
                        mmgroup(p[:, ti * 128:(ti + 1) * 128], [(kv[:, k, tt * 128:(tt + 1) * 128], w[:, k, 128:256]) for k in range(KC)], [w, kv], [p])
                    srcv = p[:, 0:nt_ * 128].rearrange("p (t d) -> p t d", t=nt_)
                    if (t0 // 4) % 2 == 0:
                        S.op("act", lambda e, vb=vb, t0=t0, nt_=nt_, srcv=srcv: e.copy(out=vb[:, t0:t0 + nt_, :], in_=srcv), reads=[p], writes=[(vb, t0)])
                    else:
                        S.op("dve", lambda e, vb=vb, t0=t0, nt_=nt_, srcv=srcv: e.tensor_copy(out=vb[:, t0:t0 + nt_, :], in_=srcv), reads=[p], writes=[(vb, t0)])
                S.dma("sp", mv.ap[h], vb[:, :, :], reads=[vb], writes=[(mv, h)])
            flush()
        if stop_after == "S4a":
            return nc

        def attn_core(chunks, nq, exp_scale, pt, o_ps, d_ps, sbanks, finish, dacc):
            n = slice(0, nq)
            nch = len(chunks)

            def qk(c):
                chunks[c][0](sbanks[c % len(sbanks)])

            qk(0)
            if nch > 1:
                qk(1)
            for c in range(nch):
                if c + 2 < nch:
                    qk(c + 2)
                sb = sbanks[c % len(sbanks)]
                p_ = pt[c % len(pt)]
                S.op("act", lambda e, sb=sb, p_=p_: e.activation(out=p_[:, n], in_=sb[:, n], func=AF.Exp, scale=exp_scale),
                     reads=[sb], writes=[p_])
                v_ap, v_reads = chunks[c][1], chunks[c][2]
                S.op("pe", lambda e, v_ap=v_ap, p_=p_, c=c: e.matmul(o_ps[:, n], lhsT=v_ap, rhs=p_[:, n], start=(c == 0), stop=(c == nch - 1)),
                     reads=v_reads + [p_], writes=[o_ps], inc=False)
                S.op("pe", lambda e, p_=p_, c=c: e.matmul(d_ps[:, n], lhsT=onesb[:, :], rhs=p_[:, n], start=(c == 0), stop=(c == nch - 1)),
                     reads=[onesb, p_], writes=[d_ps], inc=True)
            finish()

        def ada_units(l, st, bank):
            wt = [S.sbuf(f"adabg_w{i}", [128, DC, 512], BF16, st) for i in range(2)]
            bt = [S.sbuf(f"adabg_b{i}", [2, 512], F32, st) for i in range(2)]
            rt = [S.sbuf(f"adabg_r{i}", [2, 512], F32, st) for i in range(2)]
            wsrc = I["ada_w"].ap[l * D:(l + 1) * D, :].rearrange("(k p) n -> p k n", p=128)
            units = []
            for nt in range(6 * D // 512):
                def unit(nt=nt):
                    b = nt % 2
                    cs = slice(nt * 512, (nt + 1) * 512)
                    S.dma("pool", wt[b][:, :, :], wsrc[:, :, cs], reads=[I["ada_w"]], writes=[wt[b]])
                    S.dma("sp", bt[b][:, :], I["ada_b"].ap[l:l + 1, cs].partition_broadcast(2), reads=[I["ada_b"]], writes=[bt[b]])
                    for k in range(DC):
                        S.op("pe", lambda e, k=k: e.matmul(bank[0:2, :], lhsT=scb[:, k, :], rhs=wt[b][:, k, :], start=(k == 0), stop=(k == DC - 1)),
                             reads=[scb, wt[b]], writes=[bank], inc=(k == DC - 1))
                    S.op("dve", lambda e: e.tensor_tensor(out=rt[b][:, :], in0=bank[0:2, :], in1=bt[b][:, :], op=ALU.add),
                         reads=[bank, bt[b]], writes=[rt[b]])
                    S.dma("sp", modrow.ap[2 * l:2 * l + 2, cs], rt[b][:, :], reads=[rt[b]], writes=[(modrow, (l, "bg", nt))])
                units.append(unit)
            return units

        with ExitStack() as st:
            Kn = [S.sbuf(f"Kn{i}", [128, TA], BF16, st) for i in range(2)]
            Kr = [S.sbuf(f"Kr{i}", [64, TA], BF16, st) for i in range(2)]
            Vh = [S.sbuf(f"Vh{i}", [128, TAT, 128], BF16, st) for i in range(2)]
            Qn = [S.sbuf(f"Qn{i}", [128, TQ], BF16, st) for i in range(2)]
            Qr = [S.sbuf(f"Qr{i}", [64, TQ], BF16, st) for i in range(2)]
            pt = [S.sbuf(f"pt{i}", [128, NT], BF16, st) for i in range(6)]
            rden = [S.sbuf(f"rden{i}", [128, NT], F32, st) for i in range(2)]
            ob = [S.sbuf(f"ob{i}", [128, NT], BF16, st) for i in range(2)]
            dac = [S.sbuf(f"dac{i}", [128, NT], F32, st) for i in range(4)]
            bg = ada_units(1, st, PS[7])
            nunits_per = -(-len(bg) // (C.MLAH * NQT))
            cnt = 0
            def mla_loads(h):
                b = h % 2
                S.dma("sp", Kn[b][:, :], mkT.ap[h, 0:128, :], reads=[mkT], writes=[Kn[b]])
                S.dma("sp", Kr[b][:, :], mkT.ap[h, 128:192, :], reads=[mkT], writes=[Kr[b]])
                S.dma("sp", Vh[b][:, :, :], mv.ap[h], reads=[mv], writes=[Vh[b]])
                S.dma("sp", Qn[b][:, :], mqT.ap[h, 0:128, :], reads=[mqT], writes=[Qn[b]])
                S.dma("sp", Qr[b][:, :], mqT.ap[h, 128:192, :], reads=[mqT], writes=[Qr[b]])
            mla_loads(0)
            for h in range(C.MLAH):
                b = h % 2
                if h + 1 < C.MLAH:
                    mla_loads(h + 1)
                for t in range(NQT):
                    cnt += 1
                    o_ps, d_ps = PS[4 + cnt % 2], PS[6]
                    for _ in range(nunits_per):
                        if bg:
                            bg.pop(0)()
                    ts_ = slice(t * NT, (t + 1) * NT)
                    chunks = []
                    for c in range(TAT):
                        def emit(sb, c=c, b=b, ts_=ts_):
                            cs = slice(c * 128, (c + 1) * 128)
                            S.op("pe", lambda e: e.matmul(sb[:, 0:NT], lhsT=Kn[b][:, cs], rhs=Qn[b][:, ts_], start=True, stop=False),
                                 reads=[Kn[b], Qn[b]], writes=[sb], inc=False)
                            S.op("pe", lambda e: e.matmul(sb[:, 0:NT], lhsT=Kr[b][:, cs], rhs=Qr[b][:, ts_], start=False, stop=True),
                                 reads=[Kr[b], Qr[b]], writes=[sb], inc=True)
                        chunks.append((emit, Vh[b][:, c, :], [Vh[b]]))

                    def finish(o_ps=o_ps, d_ps=d_ps, cnt=cnt, h=h, ts_=ts_):
                        r_, o_ = rden[cnt % 2], ob[cnt % 2]
                        S.op("dve", lambda e: e.reciprocal(out=r_[:, :], in_=d_ps[:, 0:NT]), reads=[d_ps], writes=[r_])
                        S.op("dve", lambda e: e.tensor_tensor(out=o_[:, :], in0=o_ps[:, 0:NT], in1=r_[:, :], op=ALU.mult), reads=[o_ps, r_], writes=[o_])
                        S.dma("sp", attnT.ap[:, C.NAH + h, ts_], o_[:, :], reads=[o_], writes=[(attnT, (C.NAH + h, ts_.start))])
                    attn_core(chunks, NT, 192 ** -0.5, pt, o_ps, d_ps, PS[0:4], finish, dac[2 * (cnt % 2):2 * (cnt % 2) + 2])
            flush()

        NCTX = C.CTX // 128
        TKWT = C.TKW // 128
        with ExitStack() as st:
            Kh = [S.sbuf(f"Kh{i}", [128, TK], BF16, st) for i in range(2)]
            Vn = [S.sbuf(f"Vn{i}", [128, TKT, 128], BF16, st) for i in range(2)]
            Qh = [S.sbuf(f"Qh{i}", [128, TQ], BF16, st) for i in range(2)]
            Qs = [S.sbuf(f"Qs{i}", [128, TQ], BF16, st) for i in range(2)]
            bt = [S.sbuf(f"nab{i}", [128, 7, NT], BF16, st) for i in range(2)]
            pt = [S.sbuf(f"ptn{i}", [128, NT], BF16, st) for i in range(6)]
            rden = [S.sbuf(f"rdn{i}", [128, NT], F32, st) for i in range(2)]
            ob = [S.sbuf(f"obn{i}", [128, NT], BF16, st) for i in range(2)]
            dac = [S.sbuf(f"dacn{i}", [128, NT], F32, st) for i in range(4)]
            cnt = 0
            def na_loads(h):
                b = h % 2
                S.dma("sp", Kh[b][:, :], nakT.ap[h], reads=[nakT], writes=[Kh[b]])
                S.dma("sp", Vn[b][:, :, :], nav.ap[h], reads=[nav], writes=[Vn[b]])
                S.dma("sp", Qh[b][:, :], naqT.ap[h], reads=[naqT], writes=[Qh[b]])
                S.op("act", lambda e, b=b: e.mul(out=Qs[b][:, :], in_=Qh[b][:, :], mul=128 ** -0.5), reads=[Qh[b]], writes=[Qs[b]])
            na_loads(0)
            for h in range(C.NAH):
                b = h % 2
                if h + 1 < C.NAH:
                    na_loads(h + 1)
                for blk in range(NQT):
                    cnt += 1
                    bb = bt[cnt % 2]
                    r0_ = ((blk * C.NAH + h) * 7) * 128
                    S.dma("pool", bb[:, :, :], I["na_bias"].ap[r0_:r0_ + 7 * 128, :].rearrange("(c p) n -> p c n", p=128),
                          reads=[I["na_bias"]], writes=[bb])
                    o_ps, d_ps = PS[4 + cnt % 2], PS[6 + cnt % 2]
                    ts_ = slice(blk * NT, (blk + 1) * NT)
                    chunks = []
                    for c in range(7 + NCTX):
                        kc = (3 * blk + c) if c < 7 else (TKWT + c - 7)

                        def emit(sb, c=c, kc=kc, b=b, ts_=ts_, bb=bb):
                            cs = slice(kc * 128, (kc + 1) * 128)
                            if c < 7:
                                S.op("pe", lambda e: e.matmul(sb[:, 0:NT], lhsT=Kh[b][:, cs], rhs=Qs[b][:, ts_], start=True, stop=False),
                                     reads=[Kh[b], Qs[b]], writes=[sb], inc=False)
                                S.op("pe", lambda e: e.matmul(sb[:, 0:NT], lhsT=identb[:, :], rhs=bb[:, c, :], start=False, stop=True),
                                     reads=[identb, bb], writes=[sb], inc=True)
                            else:
                                S.op("pe", lambda e: e.matmul(sb[:, 0:NT], lhsT=Kh[b][:, cs], rhs=Qs[b][:, ts_], start=True, stop=True),
                                     reads=[Kh[b], Qs[b]], writes=[sb], inc=True)
                        chunks.append((emit, Vn[b][:, kc, :], [Vn[b]]))

                    def finish(o_ps=o_ps, d_ps=d_ps, cnt=cnt, h=h, ts_=ts_):
                        r_, o_ = rden[cnt % 2], ob[cnt % 2]
                        S.op("dve", lambda e: e.reciprocal(out=r_[:, :], in_=d_ps[:, 0:NT]), reads=[d_ps], writes=[r_])
                        S.op("dve", lambda e: e.tensor_tensor(out=o_[:, :], in0=o_ps[:, 0:NT], in1=r_[:, :], op=ALU.mult), reads=[o_ps, r_], writes=[o_])
                        S.dma("sp", attnT.ap[:, h, ts_], o_[:, :], reads=[o_], writes=[(attnT, (h, ts_.start))])
                    attn_core(chunks, NT, 1.0, pt, o_ps, d_ps, PS[0:4], finish, dac[2 * (cnt % 2):2 * (cnt % 2) + 2])
            flush()

        def lin_tm_residual(st, aT_dram, wsrc, nK, gate_row_ap, base_fn, bias_row_ap=None, bias_buf=None, tok0=0, ntok=None):
            ntok = ntok or TQ
            aT = S.sbuf("aT_sb", [128, nK, ntok], BF16, st)
            for k in range(nK):
                S.dma("sp", aT[:, k, :], aT_dram.ap[:, k, 0:ntok], reads=[aT_dram], writes=[(aT, k)])
            gt = S.sbuf("gate_bc", [128, D], F32, st)
            bcast_row(gt, gate_row_ap, modrow)
            bt_ = None
            if bias_row_ap is not None:
                bt_ = S.sbuf("bias_bc", [128, D], F32, st)
                bcast_row(bt_, bias_row_ap, bias_buf)
            wt = [S.sbuf(f"wtm{i}", [128, nK, 512], BF16, st) for i in range(2)]
            xt = [S.sbuf(f"xtm{i}", [128, 512], F32, st) for i in range(3)]
            cnt = 0
            for nb in range(D // 512):
                cs = slice(nb * 512, (nb + 1) * 512)
                w = wt[nb % 2]
                S.dma("pool", w[:, :, :], wsrc[:, cs].rearrange("(k p) n -> p k n", p=128), reads=[], writes=[w])
                for tt in range(ntok // 128):
                    cnt += 1
                    p = PS[cnt % 4]
                    x_ = xt[cnt % 3]
                    src_buf, src_key, src_ap = base_fn(tt, nb, cs)
                    S.dma("sp", x_[:, :], src_ap, reads=[(src_buf, src_key)], writes=[x_])
                    mmgroup(p[:, :], [(aT[:, k, tt * 128:(tt + 1) * 128], w[:, k, :]) for k in range(nK)], [aT, w], [p])
                    if bt_ is not None:
                        S.op("dve", lambda e, p=p: e.tensor_tensor(out=p[:, :], in0=p[:, :], in1=bt_[:, cs], op=ALU.add), reads=[p, bt_], writes=[p])
                    S.op("dve", lambda e, p=p, cs=cs: e.tensor_tensor(out=p[:, :], in0=p[:, :], in1=gt[:, cs], op=ALU.mult), reads=[p, gt], writes=[p])
                    S.op("dve", lambda e, p=p, x_=x_: e.tensor_tensor(out=x_[:, :], in0=p[:, :], in1=x_[:, :], op=ALU.add), reads=[p, x_], writes=[x_])
                    S.dma("sp", xres.ap[tok0 + tt * 128:tok0 + (tt + 1) * 128, cs], x_[:, :], reads=[x_], writes=[(xres, (tok0, tt, nb))])

        with ExitStack() as st:
            lin_tm_residual(st, attnT, I["w_out"].ap, DC, modrow.ap[0:1, 2 * D:3 * D],
                            lambda tt, nb, cs: (I["x_kv"], None, I["x_kv"].ap[C.QOFF + tt * 128:C.QOFF + (tt + 1) * 128, cs]))
            flush()
        if stop_after == "S6":
            return nc

        def mlp(l, tok0, ntok):
            NTm = 512 if ntok % 512 == 0 else NT
            NQm = ntok // NTm
            with ExitStack() as st2:
                norm_stage(st2, [(xres, tok0 + i * 128) for i in range(ntok // 128)], I["g_mlp"].ap[l:l + 1, :], I["g_mlp"], l, 3, 4, 0, store_hT(hT, 0))
                flush()
            G, FC = C.G, C.FC
            NG = FC // G
            with ExitStack() as st:
                hs = S.sbuf("hT_sb", [128, DC, ntok], BF16, st)
                for k in range(DC):
                    S.dma("sp", hs[:, k, :], hT.ap[:, k, 0:ntok], reads=[hT], writes=[(hs, k)])
                hid = S.sbuf("hid", [128, G, ntok], BF16, st)
                m5 = S.sbuf("m5", [128, D], F32, st)
                bcast_row(m5, modrow.ap[2 * l:2 * l + 1, 5 * D:6 * D], modrow)
                w1t = [S.sbuf(f"w1t{i}", [128, DC, 256], BF16, st) for i in range(2)]
                w2t = [S.sbuf(f"w2t{i}", [128, G, 512], BF16, st) for i in range(2)]
                rl = [S.sbuf(f"rl{i}", [128, NTm], BF16, st) for i in range(4)]
                ya = [S.sbuf(f"ya{i}", [128, 512], F32, st) for i in range(3)]
                xa = [S.sbuf(f"xa{i}", [128, 512], F32, st) for i in range(2)]
                w1src = I["mlp_w1"].ap[l * D:(l + 1) * D, :]
                cA = cB = cW1 = cW2 = cR = 0
                for g in range(NG):
                    for blk in range(G // 2):
                        w = w1t[cW1 % 2]
                        cW1 += 1
                        c0 = (g * G + blk * 2) * 128
                        S.dma("pool", w[:, :, :], w1src[:, c0:c0 + 256].rearrange("(k p) n -> p k n", p=128), reads=[I["mlp_w1"]], writes=[w])
                        for sub in range(2):
                            ci = blk * 2 + sub
                            for t in range(NQm):
                                cA += 1
                                p = PS[cA % 2]
                                r_ = rl[cA % 4]
                                ts_ = slice(t * NTm, (t + 1) * NTm)
                                mmgroup(p[:, 0:NTm], [(w[:, k, sub * 128:(sub + 1) * 128], hs[:, k, ts_]) for k in range(DC)], [w, hs], [p])
                                S.op("act", lambda e, p=p, r_=r_: e.activation(out=r_[:, :], in_=p[:, 0:NTm], func=AF.Relu), reads=[p], writes=[r_])
                                S.op("dve", lambda e, r_=r_, ci=ci, ts_=ts_: e.tensor_tensor(out=hid[:, ci, ts_], in0=r_[:, :], in1=r_[:, :], op=ALU.mult),
                                     reads=[r_], writes=[(hid, (ci, t))])
                    for u in range(D // 512):
                        cs = slice(u * 512, (u + 1) * 512)
                        w2 = w2t[cW2 % 2]
                        cW2 += 1
                        r0_ = l * C.DFF + g * G * 128
                        S.dma("pool", w2[:, :, :], I["mlp_w2"].ap[r0_:r0_ + G * 128, cs].rearrange("(c p) n -> p c n", p=128),
                              reads=[I["mlp_w2"]], writes=[w2])
                        for tt in range(ntok // 128):
                            cB += 1
                            p = PS[2 + cB % 4]
                            y_ = ya[cB % 3]
                            rows = slice(tt * 128, (tt + 1) * 128)
                            xrows = slice(tok0 + tt * 128, tok0 + (tt + 1) * 128)
                            if g > 0:
                                S.dma("sp", y_[:, :], yacc.ap[rows, cs], reads=[(yacc, (tt, u))], writes=[y_])
                            mmgroup(p[:, :], [(hid[:, ci, rows], w2[:, ci, :]) for ci in range(G)], [hid, w2], [p])
                            if g > 0:
                                S.op("dve", lambda e, p=p, y_=y_: e.tensor_tensor(out=y_[:, :], in0=p[:, :], in1=y_[:, :], op=ALU.add), reads=[p, y_], writes=[y_])
                            else:
                                S.op("dve", lambda e, p=p, y_=y_: e.tensor_copy(out=y_[:, :], in_=p[:, :]), reads=[p], writes=[y_])
                            if g < NG - 1:
                                S.dma("sp", yacc.ap[rows, cs], y_[:, :], reads=[y_], writes=[(yacc, (tt, u))])
                            else:
                                x_ = xa[cB % 2]
                                S.dma("sp", x_[:, :], xres.ap[xrows, cs], reads=[xres], writes=[x_])
                                S.op("dve", lambda e, y_=y_, cs=cs: e.tensor_tensor(out=y_[:, :], in0=y_[:, :], in1=m5[:, cs], op=ALU.mult), reads=[y_, m5], writes=[y_])
                                S.op("dve", lambda e, y_=y_, x_=x_: e.tensor_tensor(out=x_[:, :], in0=y_[:, :], in1=x_[:, :], op=ALU.add), reads=[y_, x_], writes=[x_])
                                S.dma("sp", xres.ap[xrows, cs], x_[:, :], reads=[x_], writes=[(xres, ("m", tok0, tt, u))])
                flush()

        NO = C.RPC * 64
        mlp(0, 0, TQ)
        if stop_after == "MLP0":
            return nc

        with ExitStack() as st2:
            norm_stage(st2, [(xres, i * 128) for i in range(TQ // 128)], I["g_mix"].ap[1:2, :], I["g_mix"], 1, 0, 1, 0, store_hT(hT, 0))
            flush()
        PAD = C.CONVW // 2
        lnm = scr("lnm", [128, NO], F32)
        lnr = scr("lnr", [128, NO], F32)
        NOT = NO // 512
        with ExitStack() as st:
            hs = S.sbuf("hT_sb", [128, DC, TQ], BF16, st)
            for k in range(DC):
                S.dma("sp", hs[:, k, :], hT.ap[:, k, :], reads=[hT], writes=[(hs, k)])
            bp = S.sbuf("bpw1", [128, 2 * DC], F32, st)
            S.dma("sp", bp[:, :], I["b_pw1_l"][:, :], reads=[I["b_pw1_l"]], writes=[bp])
            cm = S.sbuf("cmask", [128, TQ], F32, st)
            S.dma("sp", cm[:, :], I["cmask"][:, :], reads=[I["cmask"]], writes=[cm])
            onesf = S.sbuf("onesf", [128, 128], F32, st)
            S.op("pool", lambda e: e.memset(onesf[:, :], 1.0), writes=[onesf])
            wdw = S.sbuf("wdw", [128, DC, C.CONVW], F32, st)
            bdw = S.sbuf("bdw", [128, DC], F32, st)
            S.dma("sp", wdw[:, :, :], I["w_dw_l"][:, :, :], reads=[I["w_dw_l"]], writes=[wdw])
            S.dma("sp", bdw[:, :], I["b_dw_l"][:, :], reads=[I["b_dw_l"]], writes=[bdw])
            wa = [S.sbuf(f"wa{i}", [128, DC, 256], BF16, st) for i in range(2)]
            wg = [S.sbuf(f"wg{i}", [128, DC, 256], BF16, st) for i in range(2)]
            gt = [S.sbuf(f"gt{i}", [128, NT], F32, st) for i in range(2)]
            up = [S.sbuf(f"up{i}", [128, TQ + 2 * PAD], F32, st) for i in range(3)]
            accA = [S.sbuf(f"caccA{i}", [128, NO], F32, st) for i in range(2)]
            accB = [S.sbuf(f"caccB{i}", [128, NO], F32, st) for i in range(2)]
            sqv = [S.sbuf(f"sqv{i}", [128, NO], F32, st) for i in range(2)]
            for i in range(3):
                S.op("pool", lambda e, i=i: e.memset(up[i][:, :], 0.0), writes=[up[i]])
            cnt = 0
            HALF = (C.CONVW + 1) // 2

            def conv_chunk(m, u_):
                b = m % 2
                aA, aB = accA[b], accB[b]
                o0 = C.OWN
                S.op("dve", lambda e: e.tensor_scalar(out=aA[:, :], in0=u_[:, o0:o0 + NO], scalar1=wdw[:, m, 0:1], scalar2=bdw[:, m:m + 1],
                                                      op0=ALU.mult, op1=ALU.add), reads=[u_, wdw, bdw], writes=[aA])
                S.op("dve", lambda e: e.tensor_scalar(out=aB[:, :], in0=u_[:, o0 + HALF:o0 + HALF + NO], scalar1=wdw[:, m, HALF:HALF + 1], scalar2=None,
                                                      op0=ALU.mult), reads=[u_, wdw], writes=[aB])
                for j in range(1, HALF):
                    S.op("dve", lambda e, j=j: e.scalar_tensor_tensor(out=aA[:, :], in0=u_[:, o0 + j:o0 + j + NO], scalar=wdw[:, m, j:j + 1], in1=aA[:, :],
                                                                      op0=ALU.mult, op1=ALU.add), reads=[u_, wdw, aA], writes=[aA])
                    j2 = HALF + j
                    if j2 < C.CONVW:
                        S.op("dve", lambda e, j2=j2: e.scalar_tensor_tensor(out=aB[:, :], in0=u_[:, o0 + j2:o0 + j2 + NO], scalar=wdw[:, m, j2:j2 + 1], in1=aB[:, :],
                                                                            op0=ALU.mult, op1=ALU.add), reads=[u_, wdw, aB], writes=[aB])
                S.op("dve", lambda e: e.tensor_tensor(out=aA[:, :], in0=aA[:, :], in1=aB[:, :], op=ALU.add), reads=[aA, aB], writes=[aA])
                S.op("act", lambda e: e.activation(out=sqv[b][:, :], in_=aA[:, :], func=AF.Square), reads=[aA], writes=[sqv[b]])
                for t in range(NOT):
                    ts_ = slice(t * 512, (t + 1) * 512)
                    S.op("pe", lambda e, t=t, ts_=ts_: e.matmul(PS[4 + t][:, :], lhsT=onesf[:, :], rhs=aA[:, ts_], start=(m == 0), stop=(m == DC - 1)),
                         reads=[onesf, aA], writes=[PS[4 + t]])
                    S.op("pe", lambda e, t=t, ts_=ts_: e.matmul(PS[4 + NOT + t][:, :], lhsT=onesf[:, :], rhs=sqv[b][:, ts_], start=(m == 0), stop=(m == DC - 1)),
                         reads=[onesf, sqv[b]], writes=[PS[4 + NOT + t]])
                S.dma("sp", uT.ap[:, m, 0:NO], aA[:, :], reads=[aA], writes=[(uT, ("v", m))])

            pending = None
            for m2 in range(DC // 2):
                a_, g_ = wa[m2 % 2], wg[m2 % 2]
                S.dma("pool", a_[:, :, :], I["w_pw1"].ap[:, m2 * 256:(m2 + 1) * 256].rearrange("(k p) n -> p k n", p=128), reads=[I["w_pw1"]], writes=[a_])
                S.dma("pool", g_[:, :, :], I["w_pw1"].ap[:, D + m2 * 256:D + (m2 + 1) * 256].rearrange("(k p) n -> p k n", p=128), reads=[I["w_pw1"]], writes=[g_])
                for sub in range(2):
                    m = 2 * m2 + sub
                    ws = slice(sub * 128, (sub + 1) * 128)
                    u_ = up[m % 3]
                    for t in range(NQT):
                        cnt += 1
                        b = cnt % 2
                        pa, pg = PS[b], PS[2 + b]
                        ts_ = slice(t * NT, (t + 1) * NT)
                        us_ = slice(PAD + t * NT, PAD + (t + 1) * NT)
                        mmgroup(pa[:, 0:NT], [(a_[:, k, ws], hs[:, k, ts_]) for k in range(DC)], [a_, hs], [pa])
                        mmgroup(pg[:, 0:NT], [(g_[:, k, ws], hs[:, k, ts_]) for k in range(DC)], [g_, hs], [pg])
                        S.op("act", lambda e, b=b, pg=pg, m=m: e.activation(out=gt[b][:, :], in_=pg[:, 0:NT], func=AF.Sigmoid, bias=bp[:, DC + m:DC + m + 1], scale=1.0),
                             reads=[pg, bp], writes=[gt[b]])
                        S.op("dve", lambda e, b=b, pa=pa, m=m, u_=u_, us_=us_: e.scalar_tensor_tensor(out=u_[:, us_], in0=pa[:, 0:NT], scalar=bp[:, m:m + 1], in1=gt[b][:, :],
                                                                                                       op0=ALU.add, op1=ALU.mult), reads=[pa, bp, gt[b]], writes=[u_])
                        S.op("dve", lambda e, u_=u_, us_=us_, ts_=ts_: e.tensor_tensor(out=u_[:, us_], in0=u_[:, us_], in1=cm[:, ts_], op=ALU.mult), reads=[u_, cm], writes=[u_])
                    if pending is not None:
                        conv_chunk(*pending)
                    pending = (m, u_)
            conv_chunk(*pending)
            mt = S.sbuf("lnmean", [128, NO], F32, st)
            vt = S.sbuf("lnvar", [128, NO], F32, st)
            for t in range(NOT):
                ts_ = slice(t * 512, (t + 1) * 512)
                p1, p2 = PS[4 + t], PS[4 + NOT + t]
                S.op("dve", lambda e, ts_=ts_, p1=p1: e.tensor_scalar(out=mt[:, ts_], in0=p1[:, :], scalar1=1.0 / D, scalar2=None, op0=ALU.mult),
                     reads=[p1], writes=[(mt, t)])
                S.op("dve", lambda e, ts_=ts_: e.tensor_tensor(out=vt[:, ts_], in0=mt[:, ts_], in1=mt[:, ts_], op=ALU.mult), reads=[(mt, t)], writes=[(vt, t)])
                S.op("dve", lambda e, ts_=ts_, p2=p2: e.scalar_tensor_tensor(out=vt[:, ts_], in0=p2[:, :], scalar=1.0 / D, in1=vt[:, ts_],
                                                                             op0=ALU.mult, op1=ALU.subtract), reads=[p2, (vt, t)], writes=[(vt, t)])
                rstd_from(vt[:, ts_], vt[:, ts_], 1.0, C.LN_EPS, [(vt, t)], [(vt, t)])
            S.dma("sp", lnm.ap[:, :], mt[:, :], reads=[mt], writes=[lnm])
            S.dma("sp", lnr.ap[:, :], vt[:, :], reads=[vt], writes=[lnr])
            flush()
        with ExitStack() as st:
            mt = S.sbuf("lnmean", [128, NO], F32, st)
            rt_ = S.sbuf("lnrstd", [128, NO], F32, st)
            S.dma("sp", mt[:, :], lnm.ap[:, :], reads=[lnm], writes=[mt])
            S.dma("sp", rt_[:, :], lnr.ap[:, :], reads=[lnr], writes=[rt_])
            gl = S.sbuf("gln", [128, DC], F32, st)
            bl = S.sbuf("bln", [128, DC], F32, st)
            S.dma("sp", gl[:, :], I["g_ln_l"][:, :], reads=[I["g_ln_l"]], writes=[gl])
            S.dma("sp", bl[:, :], I["b_ln_l"][:, :], reads=[I["b_ln_l"]], writes=[bl])
            vv = [S.sbuf(f"vv{i}", [128, NO], F32, st) for i in range(2)]
            so = [S.sbuf(f"so{i}", [128, NO], BF16, st) for i in range(2)]
            for m in range(DC):
                b = m % 2
                S.dma("sp", vv[b][:, :], uT.ap[:, m, 0:NO], reads=[uT], writes=[vv[b]])
                S.op("dve", lambda e, b=b: e.tensor_tensor(out=vv[b][:, :], in0=vv[b][:, :], in1=mt[:, :], op=ALU.subtract), reads=[vv[b], mt], writes=[vv[b]])
                S.op("dve", lambda e, b=b: e.tensor_tensor(out=vv[b][:, :], in0=vv[b][:, :], in1=rt_[:, :], op=ALU.mult), reads=[vv[b], rt_], writes=[vv[b]])
                S.op("act", lambda e, b=b, m=m: e.activation(out=so[b][:, :], in_=vv[b][:, :], func=AF.Silu, bias=bl[:, m:m + 1], scale=gl[:, m:m + 1]),
                     reads=[vv[b], bl, gl], writes=[so[b]])
                S.dma("sp", hT.ap[:, m, 0:NO], so[b][:, :], reads=[so[b]], writes=[(hT, ("s", m))])
            flush()
        with ExitStack() as st:
            lin_tm_residual(st, hT, I["w_pw2"].ap, DC, modrow.ap[2:3, 2 * D:3 * D],
                            lambda tt, nb, cs: (xres, None, xres.ap[C.OWN + tt * 128:C.OWN + (tt + 1) * 128, cs]),
                            bias_row_ap=I["b_pw2"].ap[0:1, :], bias_buf=I["b_pw2"], tok0=C.OWN, ntok=NO)
            flush()
        mlp(1, C.OWN, NO)
        S.dma("sp", out.ap[:, :], xres.ap[C.OWN:C.OWN + C.RPC * 64, :], reads=[xres], writes=[out])
        S.wait_all("pool", [out])
        S.wait_all("sp", [out])
        flush()
    return nc


_NC_CACHE = {}


def _run(C, inputs):
    maps = prep_inputs(C, inputs)
    key = (C.D, C.SEQ, C.NCORES)
    if key not in _NC_CACHE:
        _NC_CACHE[key] = build(C)
    nc = _NC_CACHE[key]
    res = run_bass_kernel_spmd(nc, maps, core_ids=list(range(C.NCORES)))
    outs = [np.asarray(r["out"], np.float32) for r in res.results]
    return np.concatenate(outs, axis=0).reshape(1, C.SEQ, C.D)


def kernel(**inputs):
    return _run(Cfg(), inputs)
```

```python
import numpy as np
from contextlib import ExitStack
import concourse.bass as bass
import concourse.mybir as mybir
from concourse.bass_utils import run_bass_kernel_spmd

F32 = mybir.dt.float32
BF16 = mybir.dt.bfloat16
AF = mybir.ActivationFunctionType
ALU = mybir.AluOpType
AX = mybir.AxisListType


class Buf:
    def __init__(self, name, ap):
        self.name = name
        self.ap = ap
        self.st = {}

    def __getitem__(self, idx):
        return self.ap[idx]


class Sched:
    ENG = ("pe", "act", "dve", "pool", "sp")

    def __init__(self, nc, stack):
        self.nc = nc
        self.stack = stack
        self.ops = {e: [] for e in self.ENG}
        self.sem = {e: stack.enter_context(nc.semaphore("ms_" + e)) for e in self.ENG}
        self.cnt = {e: 0 for e in self.ENG}
        self.NDS = 20
        self.dsem = {q: [stack.enter_context(nc.semaphore(f"dq_{q}_{i}")) for i in range(self.NDS)]
                     for q in ("sp", "pool", "act")}
        self.dcnt = {q: [0] * self.NDS for q in ("sp", "pool", "act")}
        self.drr = {q: 0 for q in ("sp", "pool", "act")}
        self.waited = {e: {} for e in self.ENG}
        self.nbuf = 0

    def sbuf(self, name, shape, dtype, stack=None):
        self.nbuf += 1
        t = (stack or self.stack).enter_context(self.nc.sbuf_tensor(f"{name}_{self.nbuf}", list(shape), dtype))
        return Buf(name, t)

    def psum(self, name, shape, dtype, stack=None):
        self.nbuf += 1
        t = (stack or self.stack).enter_context(self.nc.psum_tensor(f"{name}_{self.nbuf}", list(shape), dtype))
        return Buf(name, t)

    def dram(self, name, shape, dtype, kind="Internal"):
        t = self.nc.dram_tensor(name, list(shape), dtype, kind=kind)
        b = Buf(name, t.ap())
        b.is_dram_out = True
        return b

    def _deps(self, reads, writes):
        toks = []
        for (b, k) in reads:
            for kk in self._keys(b, k):
                s = b.st.get(kk)
                if s and s[0] is not None:
                    toks.append(s[0])
        for (b, k) in writes:
            for kk in self._keys(b, k):
                s = b.st.get(kk)
                if s:
                    if s[0] is not None:
                        toks.append(s[0])
                    toks.extend(s[1].values())
        return toks

    @staticmethod
    def _keys(b, k):
        if k is None:
            return list(b.st.keys()) + ([None] if None not in b.st else [])
        return [k, None]

    def _commit(self, reads, writes, tok):
        for (b, k) in reads:
            s = b.st.setdefault(k, [None, {}])
            old = s[1].get(tok[2])
            if old is None or old[1] < tok[1]:
                s[1][tok[2]] = tok
        for (b, k) in writes:
            if k is None:
                b.st = {None: [tok, {}]}
            else:
                b.st[k] = [tok, {}]

    def _emit_waits(self, eng, toks):
        need = {}
        for (sem, val, sid) in toks:
            if eng == "pe" and sid == "ms_pe":
                continue
            if self.waited[eng].get(sid, 0) >= val:
                continue
            if sid not in need or need[sid][1] < val:
                need[sid] = (sem, val)
        for sid, (sem, val) in need.items():
            self.waited[eng][sid] = val
            self.ops[eng].append(("wait", sem, val))

    @staticmethod
    def _norm(lst):
        out = []
        for x in lst or []:
            if isinstance(x, Buf):
                out.append((x, None))
            else:
                out.append(x)
        return out

    def op(self, eng, fn, reads=None, writes=None, inc=True):
        reads = self._norm(reads)
        writes = self._norm(writes)
        toks = self._deps(reads, writes)
        self._emit_waits(eng, toks)
        if inc:
            self.cnt[eng] += 1
            tok = (self.sem[eng], self.cnt[eng], "ms_" + eng)
            self.ops[eng].append(("op", fn, self.sem[eng]))
            self._commit(reads, writes, tok)
        else:
            tok = (self.sem[eng], self.cnt[eng] + 1, "ms_" + eng)
            self.ops[eng].append(("op", fn, None))
            self._commit(reads, writes, tok)
        return

    def dma(self, q, out_ap, in_ap, reads=None, writes=None, **kw):
        reads = self._norm(reads)
        writes = self._norm(writes)
        toks = self._deps(reads, writes)
        self._emit_waits(q, toks)
        i = self.drr[q]
        self.drr[q] = (i + 1) % self.NDS
        self.dcnt[q][i] += 16
        sem = self.dsem[q][i]
        tok = (sem, self.dcnt[q][i], f"dq_{q}_{i}")
        self.ops[q].append(("dma", out_ap, in_ap, sem, kw))
        self._commit(reads, writes, tok)

    def barrier(self):
        toks = [(self.sem[e], self.cnt[e], "ms_" + e) for e in self.ENG if self.cnt[e] > 0]
        for q in self.dsem:
            for i in range(self.NDS):
                if self.dcnt[q][i] > 0:
                    toks.append((self.dsem[q][i], self.dcnt[q][i], f"dq_{q}_{i}"))
        for e in self.ENG:
            need = []
            for (sem, val, sid) in toks:
                if self.waited[e].get(sid, 0) >= val:
                    continue
                self.waited[e][sid] = val
                self.ops[e].append(("wait", sem, val))

    def wait_all(self, eng, bufs):
        toks = []
        for b in bufs:
            for k, s in b.st.items():
                if s[0] is not None:
                    toks.append(s[0])
        self._emit_waits(eng, toks)

    def replay(self):
        nc = self.nc
        engmap = {"pe": "tensor", "act": "scalar", "dve": "vector", "pool": "gpsimd", "sp": "sync"}
        with nc.Block() as block:
            for e in self.ENG:
                ops = self.ops[e]

                def body(h, ops=ops):
                    for o in ops:
                        if o[0] == "wait":
                            h.wait_ge(o[1], o[2])
                        elif o[0] == "op":
                            ins = o[1](h)
                            if o[2] is not None:
                                ins.then_inc(o[2], 1)
                        else:
                            h.dma_start(out=o[1], in_=o[2], **o[4]).then_inc(o[3], 16)

                getattr(block, engmap[e])(body)
        self.ops = {e: [] for e in self.ENG}


class Cfg:
    def __init__(self, D=4096, SEQ=8192, NCORES=8, CTX=256, NAH=16, MLAH=16, QR=1536, KVR=512, DFF=16384):
        self.D, self.SEQ, self.NCORES, self.CTX = D, SEQ, NCORES, CTX
        self.NAH, self.MLAH, self.QR, self.KVR, self.DFF = NAH, MLAH, QR, KVR, DFF
        self.L = 2
        self.W = 64
        self.ROWS = SEQ // 64
        self.RPC = self.ROWS // NCORES
        assert self.RPC == 16
        self.DC = D // 128
        self.TQ = (self.RPC + 2) * 64
        self.TKW = (self.RPC + 10) * 64
        self.TK = self.TKW + CTX
        self.TA = SEQ + CTX
        self.QOFF = 4 * 64
        self.OWN = 64
        self.NAW = NAH * 128
        self.MLAW = MLAH * 128
        self.OFF_Q_LAT = 3 * self.NAW
        self.OFF_KV_LAT = self.OFF_Q_LAT + QR
        self.INW = self.OFF_KV_LAT + KVR + 64
        self.QC = QR // 128
        self.KC = KVR // 128
        self.FC = DFF // 128
        self.NT = 384
        self.CONVW = 31
        self.G = min(16, self.FC)
        self.EPS = 1e-6
        self.LN_EPS = 1e-5


def rope_tables(n_tok, W=64):
    t = np.arange(n_tok, dtype=np.int32)
    pos = np.stack([t // W, t % W], axis=-1).astype(np.float32)
    n_freq = 16
    inv_freq = (np.float32(10000.0) ** (-np.arange(n_freq, dtype=np.float32) / np.float32(n_freq))).astype(np.float32)
    ang = (pos[:, :, None] * inv_freq).reshape(n_tok, 2 * n_freq).astype(np.float32)
    return np.cos(ang).astype(np.float32), np.sin(ang).astype(np.float32)


def lay_pc(v, nchunk):
    return np.ascontiguousarray(np.asarray(v, np.float32).reshape(nchunk, 128).T)


def prep_inputs(C, inp):
    D, SEQ = C.D, C.SEQ
    x = np.asarray(inp["x"], np.float32).reshape(SEQ, D)
    ctx = np.asarray(inp["ctx"], np.float32).reshape(C.CTX, D)
    cos, sin = rope_tables(SEQ)
    cos2 = np.concatenate([cos.T, cos.T], 0)
    sin2 = np.concatenate([sin.T, sin.T], 0)
    cosk = np.concatenate([cos2, np.ones((64, C.CTX), np.float32)], 1)
    sink = np.concatenate([sin2, np.zeros((64, C.CTX), np.float32)], 1)
    rmat = np.zeros((64, 64), np.float32)
    for i in range(32):
        rmat[i + 32, i] = -1.0
        rmat[i, i + 32] = 1.0
    ident = np.eye(128, dtype=np.float32)
    sel = np.zeros((2, 2, 128), np.float32)
    sel[0, 0, :] = 1.0
    sel[1, 1, :] = 1.0
    rpb = np.asarray(inp["att_na_rpb"], np.float32)[0]
    gq = np.asarray(inp["att_g_mla_q"], np.float32)[0]
    gk = np.asarray(inp["att_g_mla_k"], np.float32)[0]
    shared = {
        "x_allT": np.ascontiguousarray(x.T), "ctxT": np.ascontiguousarray(ctx.T),
        "g_mix_c": np.ascontiguousarray(np.concatenate([lay_pc(np.asarray(inp["g_mix"], np.float32)[l], C.DC) for l in range(C.L)], 1)),
        "c2_l": np.ascontiguousarray(np.stack([lay_pc(np.asarray(inp["c"]).reshape(-1), C.DC),
                                               lay_pc(np.asarray(inp["c_ctx"]).reshape(-1), C.DC)], -1)).reshape(128, C.DC * 2),
        "ada_w": np.asarray(inp["ada_w"], np.float32).reshape(C.L * D, 6 * D),
        "ada_b": np.asarray(inp["ada_b"], np.float32).reshape(C.L, 6 * D),
        "g_mix": np.asarray(inp["g_mix"], np.float32).reshape(C.L, D),
        "g_mlp": np.asarray(inp["g_mlp"], np.float32).reshape(C.L, D),
        "mlp_w1": np.asarray(inp["mlp_w1"], np.float32).reshape(C.L * D, C.DFF),
        "mlp_w2": np.asarray(inp["mlp_w2"], np.float32).reshape(C.L * C.DFF, D),
        "w_in": np.asarray(inp["att_w_in"], np.float32).reshape(D, C.INW),
        "g_qa_l": lay_pc(inp["att_g_qa"][0], C.QC),
        "w_qb": np.asarray(inp["att_w_qb"], np.float32).reshape(C.QR, C.MLAH * 192),
        "g_kva_l": lay_pc(inp["att_g_kva"][0], C.KC),
        "w_kvb": np.asarray(inp["att_w_kvb"], np.float32).reshape(C.KVR, C.MLAH * 256),
        "g_na_q": np.asarray(inp["att_g_na_q"], np.float32).reshape(128, 1),
        "g_na_k": np.asarray(inp["att_g_na_k"], np.float32).reshape(128, 1),
        "g_mq_n": gq[:128].reshape(128, 1).copy(), "g_mq_r": gq[128:].reshape(64, 1).copy(),
        "g_mk_n": gk[:128].reshape(128, 1).copy(), "g_mk_r": gk[128:].reshape(64, 1).copy(),
        "w_out": np.asarray(inp["att_w_out"], np.float32).reshape(C.NAW + C.MLAW, D),
        "w_pw1": np.asarray(inp["conv_w_pw1"], np.float32).reshape(D, 2 * D),
        "b_pw1_l": lay_pc(inp["conv_b_pw1"][0], 2 * C.DC),
        "w_dw_l": np.ascontiguousarray(np.asarray(inp["conv_w_dw"], np.float32)[0].T.reshape(C.DC, 128, C.CONVW).transpose(1, 0, 2)),
        "b_dw_l": lay_pc(inp["conv_b_dw"][0], C.DC),
        "g_ln_l": lay_pc(inp["conv_g_ln"][0], C.DC),
        "b_ln_l": lay_pc(inp["conv_b_ln"][0], C.DC),
        "w_pw2": np.asarray(inp["conv_w_pw2"], np.float32).reshape(D, D),
        "b_pw2": np.asarray(inp["conv_b_pw2"], np.float32).reshape(1, D),
        "cosk": cosk, "sink": sink, "rmat": rmat, "ident": ident, "sel": sel.reshape(4, 128),
    }
    maps = []
    for ci in range(C.NCORES):
        r0 = ci * C.RPC
        m = dict(shared)
        xkv = np.zeros((C.TK, D), np.float32)
        lo, hi = (r0 - 5) * 64, (r0 + 21) * 64
        slo, shi = max(lo, 0), min(hi, SEQ)
        xkv[slo - lo: shi - lo] = x[slo:shi]
        xkv[C.TKW:] = ctx
        m["x_kv"] = xkv
        qlo = (r0 - 1) * 64
        cq = np.ones((64, C.TQ), np.float32)
        sq = np.zeros((64, C.TQ), np.float32)
        msk = np.zeros((128, C.TQ), np.float32)
        a, b = max(qlo, 0), min(qlo + C.TQ, SEQ)
        cq[:, a - qlo: b - qlo] = cos2[:, a:b]
        sq[:, a - qlo: b - qlo] = sin2[:, a:b]
        msk[:, a - qlo: b - qlo] = 1.0
        m["cosq"], m["sinq"], m["cmask"] = cq, sq, msk
        n = np.arange(384)
        p = np.arange(128)
        tab = np.empty((3, C.NAH, 7, 128, 384), np.float32)
        for b_ in range(3):
            qr = (r0 - 1 + 6 * b_ + n // 64)[None, :]
            qc = (n % 64)[None, :]
            for c_ in range(7):
                kr = (r0 - 5 + 6 * b_ + 2 * c_ + p // 64)[:, None]
                kc = (p % 64)[:, None]
                rs = np.clip(qr - 4, 0, C.ROWS - 8)
                cs = np.clip(qc - 8, 0, 64 - 16)
                valid = (kr >= 0) & (kr < C.ROWS) & (qr >= 0) & (qr < C.ROWS) & (kr >= rs) & (kr < rs + 8) \
                    & (kc >= cs) & (kc < cs + 16)
                ri = np.clip(kr - qr + 7, 0, 14)
                cidx = np.clip(kc - qc, -15, 15) + 15
                g = rpb[:, ri, cidx]
                tab[b_, :, c_] = np.where(valid[None], g, np.float32(-30000.0))
        m["na_bias"] = tab.reshape(3 * C.NAH * 7 * 128, 384)
        maps.append(m)
    return maps


def build(C, dbg=(), stop_after=None):
    nc = bass.Bass("TRN2", target_bir_lowering=False)
    D, DC, TQ, TK, TA = C.D, C.DC, C.TQ, C.TK, C.TA
    with ExitStack() as stack:
        S = Sched(nc, stack)
        I = {}

        def inp(name, shape):
            I[name] = Buf(name, nc.dram_tensor(name, list(shape), F32, kind="ExternalInput").ap())

        for name, shape in [
            ("x_allT", [D, C.SEQ]), ("ctxT", [D, C.CTX]), ("g_mix_c", [128, C.L * DC]), ("x_kv", [TK, D]), ("c2_l", [128, DC * 2]),
            ("ada_w", [C.L * D, 6 * D]), ("ada_b", [C.L, 6 * D]), ("g_mix", [C.L, D]), ("g_mlp", [C.L, D]),
            ("mlp_w1", [C.L * D, C.DFF]), ("mlp_w2", [C.L * C.DFF, D]), ("w_in", [D, C.INW]),
            ("g_qa_l", [128, C.QC]), ("w_qb", [C.QR, C.MLAH * 192]), ("g_kva_l", [128, C.KC]),
            ("w_kvb", [C.KVR, C.MLAH * 256]), ("g_na_q", [128, 1]), ("g_na_k", [128, 1]),
            ("g_mq_n", [128, 1]), ("g_mq_r", [64, 1]), ("g_mk_n", [128, 1]), ("g_mk_r", [64, 1]),
            ("w_out", [C.NAW + C.MLAW, D]), ("w_pw1", [D, 2 * D]), ("b_pw1_l", [128, 2 * DC]),
            ("w_dw_l", [128, DC, C.CONVW]), ("b_dw_l", [128, DC]), ("g_ln_l", [128, DC]), ("b_ln_l", [128, DC]),
            ("w_pw2", [D, D]), ("b_pw2", [1, D]), ("cosk", [64, TA]), ("sink", [64, TA]), ("rmat", [64, 64]),
            ("ident", [128, 128]), ("sel", [4, 128]), ("cosq", [64, TQ]), ("sinq", [64, TQ]), ("cmask", [128, TQ]),
            ("na_bias", [3 * C.NAH * 7 * 128, 384]),
        ]:
            inp(name, shape)
        out = Buf("out", nc.dram_tensor("out", [C.RPC * 64, D], F32, kind="ExternalOutput").ap())
        out.is_dram_out = True

        def scr(name, shape, dt):
            return S.dram(name, shape, dt, kind=("ExternalOutput" if name in dbg else "Internal"))

        modrow = scr("modrow", [C.L * 2, 6 * D], F32)
        kvnT = scr("kvnT", [128, C.KC, TA], BF16)
        kr0 = scr("kr0", [64, TA], F32)
        sqpe = scr("sqpe", [64, TA], BF16)
        hTkv = scr("hTkv", [128, DC, TK], BF16)
        naqT = scr("naqT", [C.NAH, 128, TQ], BF16)
        nakT = scr("nakT", [C.NAH, 128, TK], BF16)
        nav = scr("nav", [C.NAH, 128, TK // 128, 128], BF16)
        qlraw = scr("qlraw", [128, C.QC, TQ], F32)
        mqT = scr("mqT", [C.MLAH, 192, TQ], BF16)
        mkT = scr("mkT", [C.MLAH, 192, TA], BF16)
        mv = scr("mv", [C.MLAH, 128, TA // 128, 128], BF16)
        attnT = scr("attnT", [128, DC, TQ], BF16)
        xres = scr("xres", [TQ, D], F32)
        yacc = scr("yacc", [TQ, D], F32)
        hT = scr("hT", [128, DC, TQ], BF16)
        uT = scr("uT", [128, DC, TQ], F32)

        identb = S.sbuf("identb", [128, 128], BF16)
        onesb = S.sbuf("onesb", [128, 128], BF16)
        rmat = S.sbuf("rmat", [64, 64], F32)
        self_ = S.sbuf("sel", [2, 2, 128], F32)
        S.dma("pool", identb[:, :], I["ident"][:, :], reads=[I["ident"]], writes=[identb])
        onesf_p = S.sbuf("onesf_p", [128, 128], F32)
        S.op("pool", lambda e: e.memset(onesf_p[:, :], 1.0), writes=[onesf_p])
        identf = S.sbuf("identf", [128, 128], F32)
        S.dma("sp", identf[:, :], I["ident"][:, :], reads=[I["ident"]], writes=[identf])
        S.dma("sp", rmat[:, :], I["rmat"][:, :], reads=[I["rmat"]], writes=[rmat])
        S.dma("sp", self_[:, :, :], I["sel"].ap.rearrange("(a b) m -> a b m", a=2), reads=[I["sel"]], writes=[self_])
        S.op("pool", lambda e: e.memset(onesb[:, :], 1.0), writes=[onesb])
        PS = [S.psum(f"ps{i}", [128, 512], F32) for i in range(8)]

        def psb(i):
            return PS[i].ap[:, :].bitcast(BF16)

        def flush():
            S.barrier()
            S.replay()

        def bcast_row(dst, row_ap, rd):
            S.dma("sp", dst[:, :], row_ap.partition_broadcast(128), reads=[rd], writes=[dst])

        def rstd_from(dst_ap, src_ap, n, eps, reads, writes):
            S.op("act", lambda e: e.activation(out=dst_ap, in_=src_ap, func=AF.Sqrt, bias=float(eps), scale=1.0 / n),
                 reads=reads, writes=writes)
            S.op("dve", lambda e: e.reciprocal(out=dst_ap, in_=dst_ap), reads=writes, writes=writes)

        scb = S.sbuf("scb", [128, DC, 2], BF16)
        with ExitStack() as st:
            cl = S.sbuf("cl", [128, DC, 2], F32, st)
            S.dma("sp", cl[:, :, :], I["c2_l"].ap.rearrange("p (k t) -> p k t", t=2), reads=[I["c2_l"]], writes=[cl])
            S.op("act", lambda e: e.activation(out=scb[:, :, :], in_=cl[:, :, :], func=AF.Silu), reads=[cl], writes=[scb])
            CBW = min(4096, 6 * D)
            NBK = CBW // 512
            wt = [S.sbuf(f"adaw{i}", [128, CBW], BF16, st) for i in range(4)]
            bt = S.sbuf("adab", [2, CBW], F32, st)
            rt = S.sbuf("adar", [2, CBW], F32, st)
            it = 0
            for l in range(1):
                for cb in range(6 * D // CBW):
                    cs = slice(cb * CBW, (cb + 1) * CBW)
                    S.dma("sp", bt[:, :], I["ada_b"].ap[l:l + 1, cs].partition_broadcast(2), reads=[I["ada_b"]], writes=[bt])
                    for k in range(DC):
                        w = wt[it % 4]
                        it += 1
                        S.dma("pool", w[:, :], I["ada_w"].ap[l * D + k * 128:l * D + (k + 1) * 128, cs], reads=[I["ada_w"]], writes=[w])
                        for n_ in range(NBK):
                            S.op("pe", lambda e, w=w, k=k, n_=n_: e.matmul(PS[n_][0:2, :], lhsT=scb[:, k, :], rhs=w[:, n_ * 512:(n_ + 1) * 512],
                                                                          start=(k == 0), stop=(k == DC - 1)),
                                 reads=[scb, w], writes=[PS[n_]], inc=(n_ == NBK - 1))
                    for n_ in range(NBK):
                        ns = slice(n_ * 512, (n_ + 1) * 512)
                        S.op("dve", lambda e, n_=n_, ns=ns: e.tensor_tensor(out=rt[:, ns], in0=PS[n_][0:2, :], in1=bt[:, ns], op=ALU.add),
                             reads=[PS[n_], bt], writes=[(rt, n_)])
                    S.dma("sp", modrow.ap[2 * l:2 * l + 2, cs], rt[:, :], reads=[rt], writes=[(modrow, (l, cb))])
            flush()
        if stop_after == "S0":
            return nc

        def mmgroup_kt(ps_aps, lhs_fn, rhs_fn, nk, reads, writes):
            nt_ = len(ps_aps)
            for k in range(nk):
                for t in range(nt_):
                    S.op("pe", lambda e, k=k, t=t: e.matmul(ps_aps[t], lhsT=lhs_fn(k), rhs=rhs_fn(k, t), start=(k == 0), stop=(k == nk - 1)),
                         reads=reads, writes=[writes[t]], inc=(k == nk - 1))

        def mmgroup(ps_ap, pairs, reads, writes):
            n = len(pairs)
            for i, (l, r) in enumerate(pairs):
                S.op("pe", lambda e, l=l, r=r, i=i: e.matmul(ps_ap, lhsT=l, rhs=r, start=(i == 0), stop=(i == n - 1)),
                     reads=reads, writes=writes, inc=(i == n - 1))

        def norm_stage(st, tiles, gvec_ap, gbuf, l, j_shift, j_scale, variant, consume, blk=4):
            Gt = S.sbuf("Gt", [128, D], F32, st)
            St = S.sbuf("St", [128, D], F32, st)
            row = 2 * l + variant
            bcast_row(Gt, modrow.ap[row:row + 1, j_scale * D:(j_scale + 1) * D], modrow)
            bcast_row(St, gvec_ap, gbuf)
            S.op("dve", lambda e: e.scalar_tensor_tensor(out=Gt[:, :], in0=Gt[:, :], scalar=1.0, in1=St[:, :],
                                                         op0=ALU.add, op1=ALU.mult), reads=[Gt, St], writes=[Gt])
            bcast_row(St, modrow.ap[row:row + 1, j_shift * D:(j_shift + 1) * D], modrow)
            xt = [S.sbuf(f"xt{i}", [128, D], F32, st) for i in range(2)]
            hb = [S.sbuf(f"hb{i}", [128, D], BF16, st) for i in range(2)]
            ss = [S.sbuf(f"ss{i}", [128, 1], F32, st) for i in range(2)]
            rs = [S.sbuf(f"rs{i}", [128, 1], F32, st) for i in range(2)]
            hblk = [S.sbuf(f"hblk{i}", [128, DC, blk * 128], BF16, st) for i in range(2)]
            G8 = min(8, DC)
            tcount = 0
            for b0 in range(0, len(tiles), blk):
                grp = tiles[b0:b0 + blk]
                hk = hblk[(b0 // blk) % 2]
                for ti, (src, r0) in enumerate(grp):
                    b = tcount % 2
                    tcount += 1
                    S.dma("sp", xt[b][:, :], src.ap[r0:r0 + 128, :], reads=[src], writes=[xt[b]])
                    S.op("pool", lambda e, b=b: e.memset(ss[b][:, :], 0.0), writes=[ss[b]])
                    S.op("act", lambda e, b=b: e.activation(out=hb[b][:, :], in_=xt[b][:, :], func=AF.Square, accum_out=ss[b][:, :]),
                         reads=[xt[b], ss[b]], writes=[hb[b], ss[b]])
                    rstd_from(rs[b][:, :], ss[b][:, :], D, C.EPS, [ss[b]], [rs[b]])
                    S.op("dve", lambda e, b=b: e.scalar_tensor_tensor(out=xt[b][:, :], in0=xt[b][:, :], scalar=rs[b][:, 0:1], in1=Gt[:, :],
                                                                      op0=ALU.mult, op1=ALU.mult), reads=[xt[b], rs[b], Gt], writes=[xt[b]])
                    S.op("dve", lambda e, b=b: e.tensor_tensor(out=hb[b][:, :], in0=xt[b][:, :], in1=St[:, :], op=ALU.add),
                         reads=[xt[b], St], writes=[hb[b]])
                    for k0 in range(0, DC, G8):
                        bank = 6 + ((k0 // G8) % 2)
                        pv = psb(bank)
                        for kk in range(G8):
                            k = k0 + kk
                            S.op("pe", lambda e, pv=pv, kk=kk, k=k, b=b: e.transpose(out=pv[:, kk * 128:(kk + 1) * 128],
                                                                                      in_=hb[b][:, k * 128:(k + 1) * 128], identity=identb[:, :]),
                                 reads=[hb[b], identb], writes=[PS[bank]], inc=(kk == G8 - 1))
                        eng = "act" if (k0 // G8) % 2 == 0 else "dve"
                        src_v = pv[:, 0:G8 * 128].rearrange("p (k t) -> p k t", k=G8)
                        dst_v = hk[:, k0:k0 + G8, ti * 128:(ti + 1) * 128]
                        if eng == "act":
                            S.op("act", lambda e, s=src_v, d=dst_v: e.copy(out=d, in_=s), reads=[PS[bank]], writes=[(hk, ti)])
                        else:
                            S.op("dve", lambda e, s=src_v, d=dst_v: e.tensor_copy(out=d, in_=s), reads=[PS[bank]], writes=[(hk, ti)])
                consume(hk, b0 * 128, len(grp) * 128)


        def colvec(st, dst, row, j):
            tmp = S.sbuf("cv_tmp", [DC, 128], F32, st)
            S.dma("sp", tmp[:, :], modrow.ap[row:row + 1, j * D:(j + 1) * D].rearrange("o (k p) -> (o k) p", p=128), reads=[modrow], writes=[tmp])
            S.op("pe", lambda e: e.matmul(PS[7][:, 0:DC], lhsT=tmp[:, :], rhs=identf[0:DC, 0:DC], start=True, stop=True),
                 reads=[tmp, identf], writes=[PS[7]])
            S.op("dve", lambda e: e.tensor_copy(out=dst[:, :], in_=PS[7][:, 0:DC]), reads=[PS[7]], writes=[dst])

        def norm_fm_stage(st, xT, ntot, l, variant, consume, blk_tok=512):
            row = 2 * l + variant
            Gc = S.sbuf("Gc", [128, DC], F32, st)
            Sc = S.sbuf("Sc", [128, DC], F32, st)
            gc = S.sbuf("gc", [128, DC], F32, st)
            S.dma("sp", gc[:, :], I["g_mix_c"].ap[:, l * DC:(l + 1) * DC], reads=[I["g_mix_c"]], writes=[gc])
            colvec(st, Gc, row, 1)
            colvec(st, Sc, row, 0)
            S.op("dve", lambda e: e.scalar_tensor_tensor(out=Gc[:, :], in0=Gc[:, :], scalar=1.0, in1=gc[:, :], op0=ALU.add, op1=ALU.mult),
                 reads=[Gc, gc], writes=[Gc])
            xk = [S.sbuf(f"xk{i}", [128, blk_tok], F32, st) for i in range(4)]
            sqb = [S.sbuf(f"sqb{i}", [128, blk_tok], F32, st) for i in range(4)]
            sacc = [S.sbuf(f"sacc{i}", [128, blk_tok], F32, st) for i in range(4)]
            tmp = [S.sbuf(f"ntmp{i}", [128, blk_tok], F32, st) for i in range(2)]
            rsd = [S.sbuf(f"nrs{i}", [128, blk_tok], F32, st) for i in range(2)]
            hblk = [S.sbuf(f"hblkf{i}", [128, DC, blk_tok], BF16, st) for i in range(2)]
            cx = 0
            units = []
            nstep = 2 * DC

            def pump(step):
                if units and step % max(1, nstep // max(1, pump.total)) == 0:
                    units.pop(0)()
            pump.total = 1
            for bi, a0 in enumerate(range(0, ntot, blk_tok)):
                ntok = min(blk_tok, ntot - a0)
                n = slice(0, ntok)
                ssp = PS[5 + bi % 2]
                hk = hblk[bi % 2]
                r_ = rsd[bi % 2]
                aE, aO = sacc[2 * (bi % 2)], sacc[2 * (bi % 2) + 1]
                step = 0
                for k in range(DC):
                    x_ = xk[cx % 4]
                    q_ = sqb[cx % 4]
                    cx += 1
                    S.dma("sp", x_[:, n], xT.ap[k * 128:(k + 1) * 128, a0:a0 + ntok], reads=[xT], writes=[x_])
                    S.op("act", lambda e, x_=x_, q_=q_: e.activation(out=q_[:, n], in_=x_[:, n], func=AF.Square), reads=[x_], writes=[q_])
                    a_ = aE if k % 2 == 0 else aO
                    if k < 2:
                        S.op("dve", lambda e, a_=a_, q_=q_: e.tensor_copy(out=a_[:, n], in_=q_[:, n]), reads=[q_], writes=[a_])
                    else:
                        S.op("dve", lambda e, a_=a_, q_=q_: e.tensor_tensor(out=a_[:, n], in0=a_[:, n], in1=q_[:, n], op=ALU.add), reads=[a_, q_], writes=[a_])
                    pump(step)
                    step += 1
                S.op("pe", lambda e, aE=aE, ssp=ssp: e.matmul(ssp[:, n], lhsT=onesf_p[:, :], rhs=aE[:, n], start=True, stop=(DC < 2)),
                     reads=[onesf_p, aE], writes=[ssp], inc=(DC < 2))
                if DC >= 2:
                    S.op("pe", lambda e, aO=aO, ssp=ssp: e.matmul(ssp[:, n], lhsT=onesf_p[:, :], rhs=aO[:, n], start=False, stop=True),
                         reads=[onesf_p, aO], writes=[ssp], inc=True)
                rstd_from(r_[:, n], ssp[:, n], D, C.EPS, [ssp], [r_])
                for k in range(DC):
                    x_ = xk[cx % 4]
                    t_ = tmp[cx % 2]
                    cx += 1
                    S.dma("sp", x_[:, n], xT.ap[k * 128:(k + 1) * 128, a0:a0 + ntok], reads=[xT], writes=[x_])
                    S.op("dve", lambda e, x_=x_, t_=t_, r_=r_: e.tensor_tensor(out=t_[:, n], in0=x_[:, n], in1=r_[:, n], op=ALU.mult), reads=[x_, r_], writes=[t_])
                    S.op("act", lambda e, t_=t_, hk=hk, k=k: e.activation(out=hk[:, k, n], in_=t_[:, n], func=AF.Identity, bias=Sc[:, k:k + 1], scale=Gc[:, k:k + 1]),
                         reads=[t_, Sc, Gc], writes=[(hk, k)])
                    pump(step)
                    step += 1
                while units:
                    units.pop(0)()
                res_ = consume(hk, a0, ntok)
                units = list(res_) if res_ else []
                pump.total = max(1, len(units))
            while units:
                units.pop(0)()

        def store_hT(dst, col0):
            def f(hk, toff, ntok):
                a0 = col0 + toff
                S.dma("sp", dst.ap[:, :, a0:a0 + ntok], hk[:, :, 0:ntok], reads=[hk], writes=[(dst, ("c", a0))])
            return f

        KVR, KC = C.KVR, C.KC
        with ExitStack() as st:
            wkvl = S.sbuf("wkvl", [128, DC, KVR + 64], BF16, st)
            S.dma("pool", wkvl[:, :, :], I["w_in"].ap[:, C.OFF_KV_LAT:C.INW].rearrange("(k p) n -> p k n", p=128),
                  reads=[I["w_in"]], writes=[wkvl])
            gkva = S.sbuf("gkva", [128, KC], F32, st)
            gmkr = S.sbuf("gmkr", [64, 1], F32, st)
            S.dma("sp", gkva[:, :], I["g_kva_l"][:, :], reads=[I["g_kva_l"]], writes=[gkva])
            S.dma("sp", gmkr[:, :], I["g_mk_r"][:, :], reads=[I["g_mk_r"]], writes=[gmkr])
            raw = S.sbuf("raw", [128, KC, 512], F32, st)
            sq = [S.sbuf(f"sq{i}", [128, 512], BF16, st) for i in range(2)]
            kpe = S.sbuf("kpe", [64, 512], F32, st)
            sqp = S.sbuf("sqp", [64, 512], BF16, st)
            rstd = S.sbuf("rstd", [128, 512], F32, st)
            kvn_t = S.sbuf("kvn_t", [128, KC, 512], BF16, st)
            xk = S.sbuf("xk", [64, 512], F32, st)
            cst = S.sbuf("cst", [64, 512], F32, st)
            snt = S.sbuf("snt", [64, 512], F32, st)
            t1 = S.sbuf("t1", [64, 512], F32, st)
            krt = S.sbuf("krt", [64, 512], F32, st)

            def s1_consume(base):
                def f(hk, toff, ntok):
                    a0 = base + toff
                    n = slice(0, ntok)
                    us = []

                    def u_loads():
                        S.dma("sp", cst[:, n], I["cosk"].ap[:, a0:a0 + ntok], reads=[I["cosk"]], writes=[cst])
                        S.dma("sp", snt[:, n], I["sink"].ap[:, a0:a0 + ntok], reads=[I["sink"]], writes=[snt])
                    us.append(u_loads)
                    for m in range(KC):
                        def u_m(m=m):
                            p = PS[m % 2]
                            mmgroup(p[:, n], [(wkvl[:, k, m * 128:(m + 1) * 128], hk[:, k, n]) for k in range(DC)], [wkvl, hk], [p])
                            S.op("act", lambda e, p=p, m=m: e.copy(out=raw[:, m, n], in_=p[:, n]), reads=[p], writes=[(raw, m)])
                            s_ = sq[m % 2]
                            S.op("act", lambda e, p=p, s_=s_: e.activation(out=s_[:, n], in_=p[:, n], func=AF.Square), reads=[p], writes=[s_])
                            S.op("pe", lambda e, s_=s_, m=m: e.matmul(PS[2][:, n], lhsT=onesb[:, :], rhs=s_[:, n], start=(m == 0), stop=(m == KC - 1)),
                                 reads=[s_, onesb], writes=[PS[2]], inc=True)
                        us.append(u_m)

                    def u_rope_mm():
                        p = PS[3]
                        mmgroup(p[0:64, n], [(wkvl[:, k, KVR:KVR + 64], hk[:, k, n]) for k in range(DC)], [wkvl, hk], [p])
                        S.op("act", lambda e: e.copy(out=kpe[:, n], in_=p[0:64, n]), reads=[p], writes=[kpe])
                        S.op("act", lambda e: e.activation(out=sqp[:, n], in_=p[0:64, n], func=AF.Square), reads=[p], writes=[sqp])
                        S.dma("sp", sqpe.ap[:, a0:a0 + ntok], sqp[:, n], reads=[sqp], writes=[(sqpe, a0)])
                    us.append(u_rope_mm)

                    def u_norm():
                        rstd_from(rstd[:, n], PS[2][:, n], KVR, C.EPS, [PS[2]], [rstd])
                        for m in range(KC):
                            S.op("dve", lambda e, m=m: e.scalar_tensor_tensor(out=kvn_t[:, m, n], in0=raw[:, m, n], scalar=gkva[:, m:m + 1],
                                                                              in1=rstd[:, n], op0=ALU.mult, op1=ALU.mult),
                                 reads=[(raw, m), gkva, rstd], writes=[(kvn_t, m)])
                        S.dma("sp", kvnT.ap[:, :, a0:a0 + ntok], kvn_t[:, :, n], reads=[kvn_t], writes=[(kvnT, a0)])
                    us.append(u_norm)

                    def u_rope():
                        S.op("dve", lambda e: e.tensor_scalar(out=xk[:, n], in0=kpe[:, n], scalar1=gmkr[:, 0:1], scalar2=None, op0=ALU.mult),
                             reads=[kpe, gmkr], writes=[xk])
                        S.op("pe", lambda e: e.matmul(PS[4][0:64, n], lhsT=rmat[:, :], rhs=xk[:, n], start=True, stop=True),
                             reads=[rmat, xk], writes=[PS[4]])
                        S.op("dve", lambda e: e.tensor_tensor(out=t1[:, n], in0=xk[:, n], in1=cst[:, n], op=ALU.mult), reads=[xk, cst], writes=[t1])
                        S.op("dve", lambda e: e.tensor_tensor(out=krt[:, n], in0=PS[4][0:64, n], in1=snt[:, n], op=ALU.mult), reads=[PS[4], snt], writes=[krt])
                        S.op("dve", lambda e: e.tensor_tensor(out=krt[:, n], in0=krt[:, n], in1=t1[:, n], op=ALU.add), reads=[krt, t1], writes=[krt])
                        S.dma("sp", kr0.ap[:, a0:a0 + ntok], krt[:, n], reads=[krt], writes=[(kr0, a0)])
                    us.append(u_rope)
                    return us
                return f

            with ExitStack() as st2:
                norm_fm_stage(st2, I["x_allT"], C.SEQ, 0, 0, s1_consume(0))
                flush()
            with ExitStack() as st2:
                norm_fm_stage(st2, I["ctxT"], C.CTX, 0, 1, s1_consume(C.SEQ))
                flush()
        with ExitStack() as st2:
            norm_stage(st2, [(I["x_kv"], i * 128) for i in range(C.TKW // 128)], I["g_mix"].ap[0:1, :], I["g_mix"], 0, 0, 1, 0, store_hT(hTkv, 0))
            flush()
        with ExitStack() as st2:
            norm_stage(st2, [(I["x_kv"], C.TKW + i * 128) for i in range(C.CTX // 128)], I["g_mix"].ap[0:1, :], I["g_mix"], 0, 0, 1, 1, store_hT(hTkv, C.TKW))
            flush()
        if stop_after == "S1":
            return nc

        NT = C.NT
        NQT = TQ // NT
        NKT = 4
        KT = TK // NKT
        assert KT <= 512 and KT * NKT == TK
        TKT = TK // 128
        qlrstd = scr("qlrstd", [128, TQ], F32)
        with ExitStack() as st:
            hk = S.sbuf("hTkv_sb", [128, DC, TK], BF16, st)
            for k in range(DC):
                S.dma("sp", hk[:, k, :], hTkv.ap[:, k, :], reads=[hTkv], writes=[(hk, k)])
            NB = 2
            wt = [S.sbuf(f"winw{i}", [128, DC, 256], BF16, st) for i in range(NB)]
            gq = S.sbuf("gnaq", [128, 1], F32, st)
            gk = S.sbuf("gnak", [128, 1], F32, st)
            S.dma("sp", gq[:, :], I["g_na_q"][:, :], reads=[I["g_na_q"]], writes=[gq])
            S.dma("sp", gk[:, :], I["g_na_k"][:, :], reads=[I["g_na_k"]], writes=[gk])
            rawt = [S.sbuf(f"rawt{i}", [128, 512], F32, st) for i in range(2)]
            sqt = [S.sbuf(f"sqt{i}", [128, 512], BF16, st) for i in range(2)]
            rst = [S.sbuf(f"rst{i}", [128, 512], F32, st) for i in range(2)]
            outt = [S.sbuf(f"outt{i}", [128, 512], BF16, st) for i in range(2)]
            vsb = [S.sbuf(f"vsb{i}", [128, TKT, 256], BF16, st) for i in range(2)]
            cols = list(range(0, C.OFF_KV_LAT, 256))
            cnt = 0
            for bi, c0 in enumerate(cols):
                w = wt[bi % NB]
                S.dma("pool", w[:, :, :], I["w_in"].ap[:, c0:c0 + 256].rearrange("(k p) n -> p k n", p=128),
                      reads=[I["w_in"]], writes=[w])
                if 2 * C.NAW <= c0 < 3 * C.NAW:
                    vb = vsb[bi % 2]
                    for tt in range(TKT):
                        p = PS[tt % 2]
                        mmgroup(p[:, 0:256], [(hk[:, k, tt * 128:(tt + 1) * 128], w[:, k, :]) for k in range(DC)], [hk, w], [p])
                        if tt % 2 == 0:
                            S.op("act", lambda e, p=p, tt=tt, vb=vb: e.copy(out=vb[:, tt, :], in_=p[:, 0:256]), reads=[p], writes=[(vb, tt)])
                        else:
                            S.op("dve", lambda e, p=p, tt=tt, vb=vb: e.tensor_copy(out=vb[:, tt, :], in_=p[:, 0:256]), reads=[p], writes=[(vb, tt)])
                    for sub in range(2):
                        h = (c0 - 2 * C.NAW) // 128 + sub
                        S.dma("sp", nav.ap[h], vb[:, :, sub * 128:(sub + 1) * 128], reads=[vb], writes=[(nav, h)])
                    continue
                for sub in range(2):
                    mc = c0 + sub * 128
                    wsl = lambda k, w=w, sub=sub: w[:, k, sub * 128:(sub + 1) * 128]
                    if mc < 2 * C.NAW:
                        isq = mc < C.NAW
                        h = (mc if isq else mc - C.NAW) // 128
                        ntile, nsz, off = (NQT, NT, C.QOFF) if isq else (NKT, KT, 0)
                        gvec = gq if isq else gk
                        dst = naqT if isq else nakT
                        for t in range(ntile):
                            cnt += 1
                            b = cnt % 2
                            p = PS[b]
                            n = slice(0, nsz)
                            a0 = off + t * nsz
                            mmgroup(p[:, n], [(wsl(k), hk[:, k, a0:a0 + nsz]) for k in range(DC)], [hk, w], [p])
                            S.op("act", lambda e, p=p, b=b, n=n: e.copy(out=rawt[b][:, n], in_=p[:, n]), reads=[p], writes=[rawt[b]])
                            S.op("act", lambda e, p=p, b=b, n=n: e.activation(out=sqt[b][:, n], in_=p[:, n], func=AF.Square), reads=[p], writes=[sqt[b]])
                            q = PS[2 + b]
                            S.op("pe", lambda e, q=q, b=b, n=n: e.matmul(q[:, n], lhsT=onesb[:, :], rhs=sqt[b][:, n], start=True, stop=True),
                                 reads=[sqt[b], onesb], writes=[q])
                            rstd_from(rst[b][:, n], q[:, n], 128, C.EPS, [q], [rst[b]])
                            S.op("dve", lambda e, b=b, n=n, gvec=gvec: e.scalar_tensor_tensor(out=outt[b][:, n], in0=rawt[b][:, n], scalar=gvec[:, 0:1],
                                                                                             in1=rst[b][:, n], op0=ALU.mult, op1=ALU.mult),
                                 reads=[rawt[b], gvec, rst[b]], writes=[outt[b]])
                            S.dma("sp", dst.ap[h, :, t * nsz:(t + 1) * nsz], outt[b][:, n], reads=[outt[b]], writes=[(dst, (h, t))])
                    else:
                        j = (mc - C.OFF_Q_LAT) // 128
                        for t in range(NQT):
                            cnt += 1
                            b = cnt % 2
                            p = PS[b]
                            n = slice(0, NT)
                            a0 = C.QOFF + t * NT
                            mmgroup(p[:, n], [(wsl(k), hk[:, k, a0:a0 + NT]) for k in range(DC)], [hk, w], [p])
                            S.op("act", lambda e, p=p, b=b, n=n: e.copy(out=rawt[b][:, n], in_=p[:, n]), reads=[p], writes=[rawt[b]])
                            S.op("act", lambda e, p=p, b=b, n=n: e.activation(out=sqt[b][:, n], in_=p[:, n], func=AF.Square), reads=[p], writes=[sqt[b]])
                            q = PS[5 + t]
                            S.op("pe", lambda e, q=q, b=b, n=n, j=j: e.matmul(q[:, n], lhsT=onesb[:, :], rhs=sqt[b][:, n], start=(j == 0), stop=(j == C.QC - 1)),
                                 reads=[sqt[b], onesb], writes=[q])
                            S.dma("sp", qlraw.ap[:, j, t * NT:(t + 1) * NT], rawt[b][:, n], reads=[rawt[b]], writes=[(qlraw, (j, t))])
            for t in range(NQT):
                rstd_from(rst[t % 2][:, 0:NT], PS[5 + t][:, 0:NT], C.QR, C.EPS, [PS[5 + t]], [rst[t % 2]])
                S.dma("sp", qlrstd.ap[:, t * NT:(t + 1) * NT], rst[t % 2][:, 0:NT], reads=[rst[t % 2]], writes=[(qlrstd, t)])
            flush()
        if stop_after == "S2":
            return nc

        QC = C.QC
        with ExitStack() as st:
            qn = S.sbuf("qn", [128, QC, TQ], BF16, st)
            gqa = S.sbuf("gqa", [128, QC], F32, st)
            gn = S.sbuf("gmqn", [128, 1], F32, st)
            gr = S.sbuf("gmqr", [64, 1], F32, st)
            S.dma("sp", gqa[:, :], I["g_qa_l"][:, :], reads=[I["g_qa_l"]], writes=[gqa])
            S.dma("sp", gn[:, :], I["g_mq_n"][:, :], reads=[I["g_mq_n"]], writes=[gn])
            S.dma("sp", gr[:, :], I["g_mq_r"][:, :], reads=[I["g_mq_r"]], writes=[gr])
            cq = S.sbuf("cosq", [64, TQ], F32, st)
            sq_ = S.sbuf("sinq", [64, TQ], F32, st)
            S.dma("sp", cq[:, :], I["cosq"][:, :], reads=[I["cosq"]], writes=[cq])
            S.dma("sp", sq_[:, :], I["sinq"][:, :], reads=[I["sinq"]], writes=[sq_])
            with ExitStack() as st2:
                qrs = S.sbuf("qrs", [128, TQ], F32, st2)
                S.dma("sp", qrs[:, :], qlrstd.ap[:, :], reads=[qlrstd], writes=[qrs])
                rl = [S.sbuf(f"qlr{i}", [128, TQ], F32, st2) for i in range(2)]
                for j in range(QC):
                    b = j % 2
                    S.dma("sp", rl[b][:, :], qlraw.ap[:, j, :], reads=[qlraw], writes=[rl[b]])
                    S.op("dve", lambda e, b=b, j=j: e.scalar_tensor_tensor(out=qn[:, j, :], in0=rl[b][:, :], scalar=gqa[:, j:j + 1], in1=qrs[:, :],
                                                                          op0=ALU.mult, op1=ALU.mult), reads=[rl[b], gqa, qrs], writes=[(qn, j)])
                flush()
            wq = [S.sbuf(f"wq{i}", [128, QC, 192], BF16, st) for i in range(2)]
            rawn = [S.sbuf(f"rawn{i}", [128, NT], F32, st) for i in range(2)]
            rawr = [S.sbuf(f"rawr{i}", [64, NT], F32, st) for i in range(2)]
            sqn = [S.sbuf(f"sqn{i}", [128, NT], BF16, st) for i in range(2)]
            sqr = [S.sbuf(f"sqr{i}", [64, NT], BF16, st) for i in range(2)]
            rsq = [S.sbuf(f"rsq{i}", [128, NT], F32, st) for i in range(2)]
            on = [S.sbuf(f"on{i}", [128, NT], BF16, st) for i in range(2)]
            xr = [S.sbuf(f"xr{i}", [64, NT], F32, st) for i in range(2)]
            t1q = [S.sbuf(f"t1q{i}", [64, NT], F32, st) for i in range(2)]
            t2q = [S.sbuf(f"t2q{i}", [64, NT], F32, st) for i in range(2)]
            orr = [S.sbuf(f"orr{i}", [64, NT], BF16, st) for i in range(2)]
            cnt = 0
            for h in range(C.MLAH):
                w = wq[h % 2]
                S.dma("pool", w[:, :, :], I["w_qb"].ap[:, h * 192:(h + 1) * 192].rearrange("(k p) n -> p k n", p=128),
                      reads=[I["w_qb"]], writes=[w])
                for t in range(NQT):
                    cnt += 1
                    b = cnt % 2
                    ts_ = slice(t * NT, (t + 1) * NT)
                    pn, pr, pss, prot = PS[b], PS[2 + b], PS[4 + b], PS[6 + b]
                    mmgroup(pn[:, 0:NT], [(w[:, k, 0:128], qn[:, k, ts_]) for k in range(QC)], [w, qn], [pn])
                    mmgroup(pr[0:64, 0:NT], [(w[:, k, 128:192], qn[:, k, ts_]) for k in range(QC)], [w, qn], [pr])
                    S.op("act", lambda e, b=b, pn=pn: e.copy(out=rawn[b][:, :], in_=pn[:, 0:NT]), reads=[pn], writes=[rawn[b]])
                    S.op("act", lambda e, b=b, pn=pn: e.activation(out=sqn[b][:, :], in_=pn[:, 0:NT], func=AF.Square), reads=[pn], writes=[sqn[b]])
                    S.op("act", lambda e, b=b, pr=pr: e.copy(out=rawr[b][:, :], in_=pr[0:64, 0:NT]), reads=[pr], writes=[rawr[b]])
                    S.op("act", lambda e, b=b, pr=pr: e.activation(out=sqr[b][:, :], in_=pr[0:64, 0:NT], func=AF.Square), reads=[pr], writes=[sqr[b]])
                    S.op("pe", lambda e, b=b, pss=pss: e.matmul(pss[:, 0:NT], lhsT=onesb[:, :], rhs=sqn[b][:, :], start=True, stop=False),
                         reads=[sqn[b], onesb], writes=[pss], inc=False)
                    S.op("pe", lambda e, b=b, pss=pss: e.matmul(pss[:, 0:NT], lhsT=onesb[0:64, :], rhs=sqr[b][:, :], start=False, stop=True),
                         reads=[sqr[b], onesb], writes=[pss])
                    rstd_from(rsq[b][:, :], pss[:, 0:NT], 192, C.EPS, [pss], [rsq[b]])
                    S.op("dve", lambda e, b=b: e.scalar_tensor_tensor(out=on[b][:, :], in0=rawn[b][:, :], scalar=gn[:, 0:1], in1=rsq[b][:, :],
                                                                      op0=ALU.mult, op1=ALU.mult), reads=[rawn[b], gn, rsq[b]], writes=[on[b]])
                    S.dma("sp", mqT.ap[h, 0:128, ts_], on[b][:, :], reads=[on[b]], writes=[(mqT, (h, 0, t))])
                    S.op("dve", lambda e, b=b: e.scalar_tensor_tensor(out=xr[b][:, :], in0=rawr[b][:, :], scalar=gr[:, 0:1], in1=rsq[b][0:64, :],
                                                                      op0=ALU.mult, op1=ALU.mult), reads=[rawr[b], gr, rsq[b]], writes=[xr[b]])
                    S.op("pe", lambda e, b=b, prot=prot: e.matmul(prot[0:64, 0:NT], lhsT=rmat[:, :], rhs=xr[b][:, :], start=True, stop=True),
                         reads=[rmat, xr[b]], writes=[prot])
                    S.op("dve", lambda e, b=b, ts_=ts_: e.tensor_tensor(out=t1q[b][:, :], in0=xr[b][:, :], in1=cq[:, ts_], op=ALU.mult),
                         reads=[xr[b], cq], writes=[t1q[b]])
                    S.op("dve", lambda e, b=b, ts_=ts_, prot=prot: e.tensor_tensor(out=t2q[b][:, :], in0=prot[0:64, 0:NT], in1=sq_[:, ts_], op=ALU.mult),
                         reads=[prot, sq_], writes=[t2q[b]])
                    S.op("dve", lambda e, b=b: e.tensor_tensor(out=orr[b][:, :], in0=t1q[b][:, :], in1=t2q[b][:, :], op=ALU.add),
                         reads=[t1q[b], t2q[b]], writes=[orr[b]])
                    S.dma("sp", mqT.ap[h, 128:192, ts_], orr[b][:, :], reads=[orr[b]], writes=[(mqT, (h, 1, t))])
            flush()

        TAT = TA // 128
        with ExitStack() as st:
            kv = S.sbuf("kvn_sb", [128, KC, TA], BF16, st)
            for k in range(KC):
                S.dma("sp", kv[:, k, :], kvnT.ap[:, k, :], reads=[kvnT], writes=[(kv, k)])
            k0 = S.sbuf("kr0_sb", [64, TA], F32, st)
            sp_ = S.sbuf("sqpe_sb", [64, TA], BF16, st)
            S.dma("sp", k0[:, :], kr0.ap[:, :], reads=[kr0], writes=[k0])
            S.dma("sp", sp_[:, :], sqpe.ap[:, :], reads=[sqpe], writes=[sp_])
            gkn = S.sbuf("gmkn", [128, 1], F32, st)
            S.dma("sp", gkn[:, :], I["g_mk_n"][:, :], reads=[I["g_mk_n"]], writes=[gkn])
            wk = [S.sbuf(f"wk{i}", [128, KC, 256], BF16, st) for i in range(2)]
            rawk = [S.sbuf(f"rawk{i}", [128, 512], F32, st) for i in range(3)]
            sqk = [S.sbuf(f"sqk{i}", [128, 512], BF16, st) for i in range(3)]
            rsk = [S.sbuf(f"rsk{i}", [128, 512], F32, st) for i in range(3)]
            okn = [S.sbuf(f"okn{i}", [128, 512], BF16, st) for i in range(3)]
            okr = [S.sbuf(f"okr{i}", [64, 512], BF16, st) for i in range(3)]
            vh = [S.sbuf(f"vh{i}", [128, TAT, 128], BF16, st) for i in range(2)]
            cnt = 0
            for h in range(C.MLAH):
                w = wk[h % 2]
                S.dma("pool", w[:, :, :], I["w_kvb"].ap[:, h * 256:(h + 1) * 256].rearrange("(k p) n -> p k n", p=128),
                      reads=[I["w_kvb"]], writes=[w])
                pend = None
                for a0 in list(range(0, TA, 512)) + [None]:
                    if a0 is not None:
                        ntok = min(512, TA - a0)
                        n = slice(0, ntok)
                        cnt += 1
                        b = cnt % 3
                        p, pss = PS[b], PS[3 + b]
                        mmgroup(p[:, n], [(w[:, k, 0:128], kv[:, k, a0:a0 + ntok]) for k in range(KC)], [w, kv], [p])
                        S.op("act", lambda e, b=b, p=p, n=n: e.copy(out=rawk[b][:, n], in_=p[:, n]), reads=[p], writes=[rawk[b]])
                        S.op("act", lambda e, b=b, p=p, n=n: e.activation(out=sqk[b][:, n], in_=p[:, n], func=AF.Square), reads=[p], writes=[sqk[b]])
                    if pend is not None:
                        pb, pn_, pa0, pnt, ppss = pend
                        S.op("pe", lambda e, pb=pb, ppss=ppss, pn_=pn_: e.matmul(ppss[:, pn_], lhsT=onesb[:, :], rhs=sqk[pb][:, pn_], start=True, stop=False),
                             reads=[sqk[pb], onesb], writes=[ppss], inc=False)
                        S.op("pe", lambda e, ppss=ppss, pn_=pn_, pa0=pa0, pnt=pnt: e.matmul(ppss[:, pn_], lhsT=onesb[0:64, :], rhs=sp_[:, pa0:pa0 + pnt], start=False, stop=True),
                             reads=[sp_, onesb], writes=[ppss])
                        rstd_from(rsk[pb][:, pn_], ppss[:, pn_], 192, C.EPS, [ppss], [rsk[pb]])
                        S.op("dve", lambda e, pb=pb, pn_=pn_: e.scalar_tensor_tensor(out=okn[pb][:, pn_], in0=rawk[pb][:, pn_], scalar=gkn[:, 0:1], in1=rsk[pb][:, pn_],
                                                                                     op0=ALU.mult, op1=ALU.mult), reads=[rawk[pb], gkn, rsk[pb]], writes=[okn[pb]])
                        S.dma("sp", mkT.ap[h, 0:128, pa0:pa0 + pnt], okn[pb][:, pn_], reads=[okn[pb]], writes=[(mkT, (h, 0, pa0))])
                        S.op("dve", lambda e, pb=pb, pn_=pn_, pa0=pa0, pnt=pnt: e.tensor_tensor(out=okr[pb][:, pn_], in0=k0[:, pa0:pa0 + pnt], in1=rsk[pb][0:64, pn_], op=ALU.mult),
                             reads=[k0, rsk[pb]], writes=[okr[pb]])
                        S.dma("sp", mkT.ap[h, 128:192, pa0:pa0 + pnt], okr[pb][:, pn_], reads=[okr[pb]], writes=[(mkT, (h, 1, pa0))])
                    pend = (b, n, a0, ntok, pss) if a0 is not None else None
                vb = vh[h % 2]
                for t0 in range(0, TAT, 4):
                    nt_ = min(4, TAT - t0)
                    cnt += 1
                    p = PS[6 + cnt % 2]
                    for ti in range(nt_):
                        tt = t0 + ti
                        mmgroup(p[:, ti * 128:(ti + 1) * 128], [(kv[:, k, tt * 128:(tt + 1) * 128], w[:, k, 128:256]) for k in range(KC)], [w, kv], [p])
                    srcv = p[:, 0:nt_ * 128].rearrange("p (t d) -> p t d", t=nt_)
                    if (t0 // 4) % 2 == 0:
                        S.op("act", lambda e, vb=vb, t0=t0, nt_=nt_, srcv=srcv: e.copy(out=vb[:, t0:t0 + nt_, :], in_=srcv), reads=[p], writes=[(vb, t0)])
                    else:
                        S.op("dve", lambda e, vb=vb, t0=t0, nt_=nt_, srcv=srcv: e.tensor_copy(out=vb[:, t0:t0 + nt_, :], in_=srcv), reads=[p], writes=[(vb, t0)])
                S.dma("sp", mv.ap[h], vb[:, :, :], reads=[vb], writes=[(mv, h)])
            flush()
        if stop_after == "S4a":
            return nc

        def attn_core(chunks, nq, exp_scale, pt, o_ps, d_ps, sbanks, finish, dacc):
            n = slice(0, nq)
            nch = len(chunks)

            def qk(c):
                chunks[c][0](sbanks[c % len(sbanks)])

            qk(0)
            if nch > 1:
                qk(1)
            for c in range(nch):
                if c + 2 < nch:
                    qk(c + 2)
                sb = sbanks[c % len(sbanks)]
                p_ = pt[c % len(pt)]
                S.op("act", lambda e, sb=sb, p_=p_: e.activation(out=p_[:, n], in_=sb[:, n], func=AF.Exp, scale=exp_scale),
                     reads=[sb], writes=[p_])
                v_ap, v_reads = chunks[c][1], chunks[c][2]
                S.op("pe", lambda e, v_ap=v_ap, p_=p_, c=c: e.matmul(o_ps[:, n], lhsT=v_ap, rhs=p_[:, n], start=(c == 0), stop=(c == nch - 1)),
                     reads=v_reads + [p_], writes=[o_ps], inc=False)
                S.op("pe", lambda e, p_=p_, c=c: e.matmul(d_ps[:, n], lhsT=onesb[:, :], rhs=p_[:, n], start=(c == 0), stop=(c == nch - 1)),
                     reads=[onesb, p_], writes=[d_ps], inc=True)
            finish()

        def ada_units(l, st, bank):
            wt = [S.sbuf(f"adabg_w{i}", [128, DC, 512], BF16, st) for i in range(2)]
            bt = [S.sbuf(f"adabg_b{i}", [2, 512], F32, st) for i in range(2)]
            rt = [S.sbuf(f"adabg_r{i}", [2, 512], F32, st) for i in range(2)]
            wsrc = I["ada_w"].ap[l * D:(l + 1) * D, :].rearrange("(k p) n -> p k n", p=128)
            units = []
            for nt in range(6 * D // 512):
                def unit(nt=nt):
                    b = nt % 2
                    cs = slice(nt * 512, (nt + 1) * 512)
                    S.dma("pool", wt[b][:, :, :], wsrc[:, :, cs], reads=[I["ada_w"]], writes=[wt[b]])
                    S.dma("sp", bt[b][:, :], I["ada_b"].ap[l:l + 1, cs].partition_broadcast(2), reads=[I["ada_b"]], writes=[bt[b]])
                    for k in range(DC):
                        S.op("pe", lambda e, k=k: e.matmul(bank[0:2, :], lhsT=scb[:, k, :], rhs=wt[b][:, k, :], start=(k == 0), stop=(k == DC - 1)),
                             reads=[scb, wt[b]], writes=[bank], inc=(k == DC - 1))
                    S.op("dve", lambda e: e.tensor_tensor(out=rt[b][:, :], in0=bank[0:2, :], in1=bt[b][:, :], op=ALU.add),
                         reads=[bank, bt[b]], writes=[rt[b]])
                    S.dma("sp", modrow.ap[2 * l:2 * l + 2, cs], rt[b][:, :], reads=[rt[b]], writes=[(modrow, (l, "bg", nt))])
                units.append(unit)
            return units

        with ExitStack() as st:
            Kn = [S.sbuf(f"Kn{i}", [128, TA], BF16, st) for i in range(2)]
            Kr = [S.sbuf(f"Kr{i}", [64, TA], BF16, st) for i in range(2)]
            Vh = [S.sbuf(f"Vh{i}", [128, TAT, 128], BF16, st) for i in range(2)]
            Qn = [S.sbuf(f"Qn{i}", [128, TQ], BF16, st) for i in range(2)]
            Qr = [S.sbuf(f"Qr{i}", [64, TQ], BF16, st) for i in range(2)]
            pt = [S.sbuf(f"pt{i}", [128, NT], BF16, st) for i in range(6)]
            rden = [S.sbuf(f"rden{i}", [128, NT], F32, st) for i in range(2)]
            ob = [S.sbuf(f"ob{i}", [128, NT], BF16, st) for i in range(2)]
            dac = [S.sbuf(f"dac{i}", [128, NT], F32, st) for i in range(4)]
            bg = ada_units(1, st, PS[7])
            nunits_per = -(-len(bg) // (C.MLAH * NQT))
            cnt = 0
            def mla_loads(h):
                b = h % 2
                S.dma("sp", Kn[b][:, :], mkT.ap[h, 0:128, :], reads=[mkT], writes=[Kn[b]])
                S.dma("sp", Kr[b][:, :], mkT.ap[h, 128:192, :], reads=[mkT], writes=[Kr[b]])
                S.dma("sp", Vh[b][:, :, :], mv.ap[h], reads=[mv], writes=[Vh[b]])
                S.dma("sp", Qn[b][:, :], mqT.ap[h, 0:128, :], reads=[mqT], writes=[Qn[b]])
                S.dma("sp", Qr[b][:, :], mqT.ap[h, 128:192, :], reads=[mqT], writes=[Qr[b]])
            mla_loads(0)
            for h in range(C.MLAH):
                b = h % 2
                if h + 1 < C.MLAH:
                    mla_loads(h + 1)
                for t in range(NQT):
                    cnt += 1
                    o_ps, d_ps = PS[4 + cnt % 2], PS[6]
                    for _ in range(nunits_per):
                        if bg:
                            bg.pop(0)()
                    ts_ = slice(t * NT, (t + 1) * NT)
                    chunks = []
                    for c in range(TAT):
                        def emit(sb, c=c, b=b, ts_=ts_):
                            cs = slice(c * 128, (c + 1) * 128)
                            S.op("pe", lambda e: e.matmul(sb[:, 0:NT], lhsT=Kn[b][:, cs], rhs=Qn[b][:, ts_], start=True, stop=False),
                                 reads=[Kn[b], Qn[b]], writes=[sb], inc=False)
                            S.op("pe", lambda e: e.matmul(sb[:, 0:NT], lhsT=Kr[b][:, cs], rhs=Qr[b][:, ts_], start=False, stop=True),
                                 reads=[Kr[b], Qr[b]], writes=[sb], inc=True)
                        chunks.append((emit, Vh[b][:, c, :], [Vh[b]]))

                    def finish(o_ps=o_ps, d_ps=d_ps, cnt=cnt, h=h, ts_=ts_):
                        r_, o_ = rden[cnt % 2], ob[cnt % 2]
                        S.op("dve", lambda e: e.reciprocal(out=r_[:, :], in_=d_ps[:, 0:NT]), reads=[d_ps], writes=[r_])
                        S.op("dve", lambda e: e.tensor_tensor(out=o_[:, :], in0=o_ps[:, 0:NT], in1=r_[:, :], op=ALU.mult), reads=[o_ps, r_], writes=[o_])
                        S.dma("sp", attnT.ap[:, C.NAH + h, ts_], o_[:, :], reads=[o_], writes=[(attnT, (C.NAH + h, ts_.start))])
                    attn_core(chunks, NT, 192 ** -0.5, pt, o_ps, d_ps, PS[0:4], finish, dac[2 * (cnt % 2):2 * (cnt % 2) + 2])
            flush()

        NCTX = C.CTX // 128
        TKWT = C.TKW // 128
        with ExitStack() as st:
            Kh = [S.sbuf(f"Kh{i}", [128, TK], BF16, st) for i in range(2)]
            Vn = [S.sbuf(f"Vn{i}", [128, TKT, 128], BF16, st) for i in range(2)]
            Qh = [S.sbuf(f"Qh{i}", [128, TQ], BF16, st) for i in range(2)]
            Qs = [S.sbuf(f"Qs{i}", [128, TQ], BF16, st) for i in range(2)]
            bt = [S.sbuf(f"nab{i}", [128, 7, NT], BF16, st) for i in range(2)]
            pt = [S.sbuf(f"ptn{i}", [128, NT], BF16, st) for i in range(6)]
            rden = [S.sbuf(f"rdn{i}", [128, NT], F32, st) for i in range(2)]
            ob = [S.sbuf(f"obn{i}", [128, NT], BF16, st) for i in range(2)]
            dac = [S.sbuf(f"dacn{i}", [128, NT], F32, st) for i in range(4)]
            cnt = 0
            def na_loads(h):
                b = h % 2
                S.dma("sp", Kh[b][:, :], nakT.ap[h], reads=[nakT], writes=[Kh[b]])
                S.dma("sp", Vn[b][:, :, :], nav.ap[h], reads=[nav], writes=[Vn[b]])
                S.dma("sp", Qh[b][:, :], naqT.ap[h], reads=[naqT], writes=[Qh[b]])
                S.op("act", lambda e, b=b: e.mul(out=Qs[b][:, :], in_=Qh[b][:, :], mul=128 ** -0.5), reads=[Qh[b]], writes=[Qs[b]])
            na_loads(0)
            for h in range(C.NAH):
                b = h % 2
                if h + 1 < C.NAH:
                    na_loads(h + 1)
                for blk in range(NQT):
                    cnt += 1
                    bb = bt[cnt % 2]
                    r0_ = ((blk * C.NAH + h) * 7) * 128
                    S.dma("pool", bb[:, :, :], I["na_bias"].ap[r0_:r0_ + 7 * 128, :].rearrange("(c p) n -> p c n", p=128),
                          reads=[I["na_bias"]], writes=[bb])
                    o_ps, d_ps = PS[4 + cnt % 2], PS[6 + cnt % 2]
                    ts_ = slice(blk * NT, (blk + 1) * NT)
                    chunks = []
                    for c in range(7 + NCTX):
                        kc = (3 * blk + c) if c < 7 else (TKWT + c - 7)

                        def emit(sb, c=c, kc=kc, b=b, ts_=ts_, bb=bb):
                            cs = slice(kc * 128, (kc + 1) * 128)
                            if c < 7:
                                S.op("pe", lambda e: e.matmul(sb[:, 0:NT], lhsT=Kh[b][:, cs], rhs=Qs[b][:, ts_], start=True, stop=False),
                                     reads=[Kh[b], Qs[b]], writes=[sb], inc=False)
                                S.op("pe", lambda e: e.matmul(sb[:, 0:NT], lhsT=identb[:, :], rhs=bb[:, c, :], start=False, stop=True),
                                     reads=[identb, bb], writes=[sb], inc=True)
                            else:
                                S.op("pe", lambda e: e.matmul(sb[:, 0:NT], lhsT=Kh[b][:, cs], rhs=Qs[b][:, ts_], start=True, stop=True),
                                     reads=[Kh[b], Qs[b]], writes=[sb], inc=True)
                        chunks.append((emit, Vn[b][:, kc, :], [Vn[b]]))

                    def finish(o_ps=o_ps, d_ps=d_ps, cnt=cnt, h=h, ts_=ts_):
                        r_, o_ = rden[cnt % 2], ob[cnt % 2]
                        S.op("dve", lambda e: e.reciprocal(out=r_[:, :], in_=d_ps[:, 0:NT]), reads=[d_ps], writes=[r_])
                        S.op("dve", lambda e: e.tensor_tensor(out=o_[:, :], in0=o_ps[:, 0:NT], in1=r_[:, :], op=ALU.mult), reads=[o_ps, r_], writes=[o_])
                        S.dma("sp", attnT.ap[:, h, ts_], o_[:, :], reads=[o_], writes=[(attnT, (h, ts_.start))])
                    attn_core(chunks, NT, 1.0, pt, o_ps, d_ps, PS[0:4], finish, dac[2 * (cnt % 2):2 * (cnt % 2) + 2])
            flush()

        def lin_tm_residual(st, aT_dram, wsrc, nK, gate_row_ap, base_fn, bias_row_ap=None, bias_buf=None, tok0=0, ntok=None):
            ntok = ntok or TQ
            aT = S.sbuf("aT_sb", [128, nK, ntok], BF16, st)
            for k in range(nK):
                S.dma("sp", aT[:, k, :], aT_dram.ap[:, k, 0:ntok], reads=[aT_dram], writes=[(aT, k)])
            gt = S.sbuf("gate_bc", [128, D], F32, st)
            bcast_row(gt, gate_row_ap, modrow)
            bt_ = None
            if bias_row_ap is not None:
                bt_ = S.sbuf("bias_bc", [128, D], F32, st)
                bcast_row(bt_, bias_row_ap, bias_buf)
            wt = [S.sbuf(f"wtm{i}", [128, nK, 512], BF16, st) for i in range(2)]
            xt = [S.sbuf(f"xtm{i}", [128, 512], F32, st) for i in range(3)]
            cnt = 0
            for nb in range(D // 512):
                cs = slice(nb * 512, (nb + 1) * 512)
                w = wt[nb % 2]
                S.dma("pool", w[:, :, :], wsrc[:, cs].rearrange("(k p) n -> p k n", p=128), reads=[], writes=[w])
                for tt in range(ntok // 128):
                    cnt += 1
                    p = PS[cnt % 4]
                    x_ = xt[cnt % 3]
                    src_buf, src_key, src_ap = base_fn(tt, nb, cs)
                    S.dma("sp", x_[:, :], src_ap, reads=[(src_buf, src_key)], writes=[x_])
                    mmgroup(p[:, :], [(aT[:, k, tt * 128:(tt + 1) * 128], w[:, k, :]) for k in range(nK)], [aT, w], [p])
                    if bt_ is not None:
                        S.op("dve", lambda e, p=p: e.tensor_tensor(out=p[:, :], in0=p[:, :], in1=bt_[:, cs], op=ALU.add), reads=[p, bt_], writes=[p])
                    S.op("dve", lambda e, p=p, cs=cs: e.tensor_tensor(out=p[:, :], in0=p[:, :], in1=gt[:, cs], op=ALU.mult), reads=[p, gt], writes=[p])
                    S.op("dve", lambda e, p=p, x_=x_: e.tensor_tensor(out=x_[:, :], in0=p[:, :], in1=x_[:, :], op=ALU.add), reads=[p, x_], writes=[x_])
                    S.dma("sp", xres.ap[tok0 + tt * 128:tok0 + (tt + 1) * 128, cs], x_[:, :], reads=[x_], writes=[(xres, (tok0, tt, nb))])

        with ExitStack() as st:
            lin_tm_residual(st, attnT, I["w_out"].ap, DC, modrow.ap[0:1, 2 * D:3 * D],
                            lambda tt, nb, cs: (I["x_kv"], None, I["x_kv"].ap[C.QOFF + tt * 128:C.QOFF + (tt + 1) * 128, cs]))
            flush()
        if stop_after == "S6":
            return nc

        def mlp(l, tok0, ntok):
            NTm = 512 if ntok % 512 == 0 else NT
            NQm = ntok // NTm
            with ExitStack() as st2:
                norm_stage(st2, [(xres, tok0 + i * 128) for i in range(ntok // 128)], I["g_mlp"].ap[l:l + 1, :], I["g_mlp"], l, 3, 4, 0, store_hT(hT, 0))
                flush()
            G, FC = C.G, C.FC
            NG = FC // G
            with ExitStack() as st:
                hs = S.sbuf("hT_sb", [128, DC, ntok], BF16, st)
                for k in range(DC):
                    S.dma("sp", hs[:, k, :], hT.ap[:, k, 0:ntok], reads=[hT], writes=[(hs, k)])
                hid = S.sbuf("hid", [128, G, ntok], BF16, st)
                m5 = S.sbuf("m5", [128, D], F32, st)
                bcast_row(m5, modrow.ap[2 * l:2 * l + 1, 5 * D:6 * D], modrow)
                w1t = [S.sbuf(f"w1t{i}", [128, DC, 256], BF16, st) for i in range(2)]
                w2t = [S.sbuf(f"w2t{i}", [128, G, 512], BF16, st) for i in range(2)]
                rl = [S.sbuf(f"rl{i}", [128, NTm], BF16, st) for i in range(4)]
                ya = [S.sbuf(f"ya{i}", [128, 512], F32, st) for i in range(3)]
                xa = [S.sbuf(f"xa{i}", [128, 512], F32, st) for i in range(2)]
                w1src = I["mlp_w1"].ap[l * D:(l + 1) * D, :]
                cA = cB = cW1 = cW2 = cR = 0
                for g in range(NG):
                    for blk in range(G // 2):
                        w = w1t[cW1 % 2]
                        cW1 += 1
                        c0 = (g * G + blk * 2) * 128
                        S.dma("pool", w[:, :, :], w1src[:, c0:c0 + 256].rearrange("(k p) n -> p k n", p=128), reads=[I["mlp_w1"]], writes=[w])
                        for sub in range(2):
                            ci = blk * 2 + sub
                            for t in range(NQm):
                                cA += 1
                                p = PS[cA % 2]
                                r_ = rl[cA % 4]
                                ts_ = slice(t * NTm, (t + 1) * NTm)
                                mmgroup(p[:, 0:NTm], [(w[:, k, sub * 128:(sub + 1) * 128], hs[:, k, ts_]) for k in range(DC)], [w, hs], [p])
                                S.op("act", lambda e, p=p, r_=r_: e.activation(out=r_[:, :], in_=p[:, 0:NTm], func=AF.Relu), reads=[p], writes=[r_])
                                S.op("dve", lambda e, r_=r_, ci=ci, ts_=ts_: e.tensor_tensor(out=hid[:, ci, ts_], in0=r_[:, :], in1=r_[:, :], op=ALU.mult),
                                     reads=[r_], writes=[(hid, (ci, t))])
                    for u in range(D // 512):
                        cs = slice(u * 512, (u + 1) * 512)
                        w2 = w2t[cW2 % 2]
                        cW2 += 1
                        r0_ = l * C.DFF + g * G * 128
                        S.dma("pool", w2[:, :, :], I["mlp_w2"].ap[r0_:r0_ + G * 128, cs].rearrange("(c p) n -> p c n", p=128),
                              reads=[I["mlp_w2"]], writes=[w2])
                        for tt in range(ntok // 128):
                            cB += 1
                            p = PS[2 + cB % 4]
                            y_ = ya[cB % 3]
                            rows = slice(tt * 128, (tt + 1) * 128)
                            xrows = slice(tok0 + tt * 128, tok0 + (tt + 1) * 128)
                            if g > 0:
                                S.dma("sp", y_[:, :], yacc.ap[rows, cs], reads=[(yacc, (tt, u))], writes=[y_])
                            mmgroup(p[:, :], [(hid[:, ci, rows], w2[:, ci, :]) for ci in range(G)], [hid, w2], [p])
                            if g > 0:
                                S.op("dve", lambda e, p=p, y_=y_: e.tensor_tensor(out=y_[:, :], in0=p[:, :], in1=y_[:, :], op=ALU.add), reads=[p, y_], writes=[y_])
                            else:
                                S.op("dve", lambda e, p=p, y_=y_: e.tensor_copy(out=y_[:, :], in_=p[:, :]), reads=[p], writes=[y_])
                            if g < NG - 1:
                                S.dma("sp", yacc.ap[rows, cs], y_[:, :], reads=[y_], writes=[(yacc, (tt, u))])
                            else:
                                x_ = xa[cB % 2]
                                S.dma("sp", x_[:, :], xres.ap[xrows, cs], reads=[xres], writes=[x_])
                                S.op("dve", lambda e, y_=y_, cs=cs: e.tensor_tensor(out=y_[:, :], in0=y_[:, :], in1=m5[:, cs], op=ALU.mult), reads=[y_, m5], writes=[y_])
                                S.op("dve", lambda e, y_=y_, x_=x_: e.tensor_tensor(out=x_[:, :], in0=y_[:, :], in1=x_[:, :], op=ALU.add), reads=[y_, x_], writes=[x_])
                                S.dma("sp", xres.ap[xrows, cs], x_[:, :], reads=[x_], writes=[(xres, ("m", tok0, tt, u))])
                flush()

        NO = C.RPC * 64
        mlp(0, 0, TQ)
        if stop_after == "MLP0":
            return nc

        with ExitStack() as st2:
            norm_stage(st2, [(xres, i * 128) for i in range(TQ // 128)], I["g_mix"].ap[1:2, :], I["g_mix"], 1, 0, 1, 0, store_hT(hT, 0))
            flush()
        PAD = C.CONVW // 2
        lnm = scr("lnm", [128, NO], F32)
        lnr = scr("lnr", [128, NO], F32)
        NOT = NO // 512
        with ExitStack() as st:
            hs = S.sbuf("hT_sb", [128, DC, TQ], BF16, st)
            for k in range(DC):
                S.dma("sp", hs[:, k, :], hT.ap[:, k, :], reads=[hT], writes=[(hs, k)])
            bp = S.sbuf("bpw1", [128, 2 * DC], F32, st)
            S.dma("sp", bp[:, :], I["b_pw1_l"][:, :], reads=[I["b_pw1_l"]], writes=[bp])
            cm = S.sbuf("cmask", [128, TQ], F32, st)
            S.dma("sp", cm[:, :], I["cmask"][:, :], reads=[I["cmask"]], writes=[cm])
            onesf = S.sbuf("onesf", [128, 128], F32, st)
            S.op("pool", lambda e: e.memset(onesf[:, :], 1.0), writes=[onesf])
            wdw = S.sbuf("wdw", [128, DC, C.CONVW], F32, st)
            bdw = S.sbuf("bdw", [128, DC], F32, st)
            S.dma("sp", wdw[:, :, :], I["w_dw_l"][:, :, :], reads=[I["w_dw_l"]], writes=[wdw])
            S.dma("sp", bdw[:, :], I["b_dw_l"][:, :], reads=[I["b_dw_l"]], writes=[bdw])
            wa = [S.sbuf(f"wa{i}", [128, DC, 256], BF16, st) for i in range(2)]
            wg = [S.sbuf(f"wg{i}", [128, DC, 256], BF16, st) for i in range(2)]
            gt = [S.sbuf(f"gt{i}", [128, NT], F32, st) for i in range(2)]
            up = [S.sbuf(f"up{i}", [128, TQ + 2 * PAD], F32, st) for i in range(3)]
            accA = [S.sbuf(f"caccA{i}", [128, NO], F32, st) for i in range(2)]
            accB = [S.sbuf(f"caccB{i}", [128, NO], F32, st) for i in range(2)]
            sqv = [S.sbuf(f"sqv{i}", [128, NO], F32, st) for i in range(2)]
            for i in range(3):
                S.op("pool", lambda e, i=i: e.memset(up[i][:, :], 0.0), writes=[up[i]])
            cnt = 0
            HALF = (C.CONVW + 1) // 2

            def conv_pieces(m, u_):
                b = m % 2
                aA, aB = accA[b], accB[b]
                o0 = C.OWN

                def taps(j0, j1):
                    for j in range(j0, j1):
                        if j == 0:
                            S.op("dve", lambda e: e.tensor_scalar(out=aA[:, :], in0=u_[:, o0:o0 + NO], scalar1=wdw[:, m, 0:1], scalar2=bdw[:, m:m + 1],
                                                                  op0=ALU.mult, op1=ALU.add), reads=[u_, wdw, bdw], writes=[aA])
                            S.op("dve", lambda e: e.tensor_scalar(out=aB[:, :], in0=u_[:, o0 + HALF:o0 + HALF + NO], scalar1=wdw[:, m, HALF:HALF + 1], scalar2=None,
                                                                  op0=ALU.mult), reads=[u_, wdw], writes=[aB])
                            continue
                        S.op("dve", lambda e, j=j: e.scalar_tensor_tensor(out=aA[:, :], in0=u_[:, o0 + j:o0 + j + NO], scalar=wdw[:, m, j:j + 1], in1=aA[:, :],
                                                                          op0=ALU.mult, op1=ALU.add), reads=[u_, wdw, aA], writes=[aA])
                        j2 = HALF + j
                        if j2 < C.CONVW:
                            S.op("dve", lambda e, j2=j2: e.scalar_tensor_tensor(out=aB[:, :], in0=u_[:, o0 + j2:o0 + j2 + NO], scalar=wdw[:, m, j2:j2 + 1], in1=aB[:, :],
                                                                                op0=ALU.mult, op1=ALU.add), reads=[u_, wdw, aB], writes=[aB])

                def tail():
                    S.op("dve", lambda e: e.tensor_tensor(out=aA[:, :], in0=aA[:, :], in1=aB[:, :], op=ALU.add), reads=[aA, aB], writes=[aA])
                    S.op("act", lambda e: e.activation(out=sqv[b][:, :], in_=aA[:, :], func=AF.Square), reads=[aA], writes=[sqv[b]])
                    for t in range(NOT):
                        ts_ = slice(t * 512, (t + 1) * 512)
                        S.op("pe", lambda e, t=t, ts_=ts_: e.matmul(PS[4 + t][:, :], lhsT=onesf[:, :], rhs=aA[:, ts_], start=(m == 0), stop=(m == DC - 1)),
                             reads=[onesf, aA], writes=[PS[4 + t]])
                        S.op("pe", lambda e, t=t, ts_=ts_: e.matmul(PS[4 + NOT + t][:, :], lhsT=onesf[:, :], rhs=sqv[b][:, ts_], start=(m == 0), stop=(m == DC - 1)),
                             reads=[onesf, sqv[b]], writes=[PS[4 + NOT + t]])
                    S.dma("sp", uT.ap[:, m, 0:NO], aA[:, :], reads=[aA], writes=[(uT, ("v", m))])
                c1 = HALF // 3
                c2 = 2 * HALF // 3
                return [lambda: taps(0, c1), lambda: taps(c1, c2), lambda: (taps(c2, HALF), tail())]

            pieces = []
            for m2 in range(DC // 2):
                a_, g_ = wa[m2 % 2], wg[m2 % 2]
                S.dma("pool", a_[:, :, :], I["w_pw1"].ap[:, m2 * 256:(m2 + 1) * 256].rearrange("(k p) n -> p k n", p=128), reads=[I["w_pw1"]], writes=[a_])
                S.dma("pool", g_[:, :, :], I["w_pw1"].ap[:, D + m2 * 256:D + (m2 + 1) * 256].rearrange("(k p) n -> p k n", p=128), reads=[I["w_pw1"]], writes=[g_])
                for sub in range(2):
                    m = 2 * m2 + sub
                    ws = slice(sub * 128, (sub + 1) * 128)
                    u_ = up[m % 3]
                    for t in range(NQT):
                        cnt += 1
                        b = cnt % 2
                        pa, pg = PS[b], PS[2 + b]
                        ts_ = slice(t * NT, (t + 1) * NT)
                        us_ = slice(PAD + t * NT, PAD + (t + 1) * NT)
                        mmgroup(pa[:, 0:NT], [(a_[:, k, ws], hs[:, k, ts_]) for k in range(DC)], [a_, hs], [pa])
                        mmgroup(pg[:, 0:NT], [(g_[:, k, ws], hs[:, k, ts_]) for k in range(DC)], [g_, hs], [pg])
                        S.op("act", lambda e, b=b, pg=pg, m=m: e.activation(out=gt[b][:, :], in_=pg[:, 0:NT], func=AF.Sigmoid, bias=bp[:, DC + m:DC + m + 1], scale=1.0),
                             reads=[pg, bp], writes=[gt[b]])
                        S.op("dve", lambda e, b=b, pa=pa, m=m, u_=u_, us_=us_: e.scalar_tensor_tensor(out=u_[:, us_], in0=pa[:, 0:NT], scalar=bp[:, m:m + 1], in1=gt[b][:, :],
                                                                                                       op0=ALU.add, op1=ALU.mult), reads=[pa, bp, gt[b]], writes=[u_])
                        S.op("dve", lambda e, u_=u_, us_=us_, ts_=ts_: e.tensor_tensor(out=u_[:, us_], in0=u_[:, us_], in1=cm[:, ts_], op=ALU.mult), reads=[u_, cm], writes=[u_])
                        if pieces:
                            pieces.pop(0)()
                    while pieces:
                        pieces.pop(0)()
                    pieces = conv_pieces(m, u_)
            while pieces:
                pieces.pop(0)()
            mt = S.sbuf("lnmean", [128, NO], F32, st)
            vt = S.sbuf("lnvar", [128, NO], F32, st)
            for t in range(NOT):
                ts_ = slice(t * 512, (t + 1) * 512)
                p1, p2 = PS[4 + t], PS[4 + NOT + t]
                S.op("dve", lambda e, ts_=ts_, p1=p1: e.tensor_scalar(out=mt[:, ts_], in0=p1[:, :], scalar1=1.0 / D, scalar2=None, op0=ALU.mult),
                     reads=[p1], writes=[(mt, t)])
                S.op("dve", lambda e, ts_=ts_: e.tensor_tensor(out=vt[:, ts_], in0=mt[:, ts_], in1=mt[:, ts_], op=ALU.mult), reads=[(mt, t)], writes=[(vt, t)])
                S.op("dve", lambda e, ts_=ts_, p2=p2: e.scalar_tensor_tensor(out=vt[:, ts_], in0=p2[:, :], scalar=1.0 / D, in1=vt[:, ts_],
                                                                             op0=ALU.mult, op1=ALU.subtract), reads=[p2, (vt, t)], writes=[(vt, t)])
                rstd_from(vt[:, ts_], vt[:, ts_], 1.0, C.LN_EPS, [(vt, t)], [(vt, t)])
            S.dma("sp", lnm.ap[:, :], mt[:, :], reads=[mt], writes=[lnm])
            S.dma("sp", lnr.ap[:, :], vt[:, :], reads=[vt], writes=[lnr])
            flush()
        with ExitStack() as st:
            mt = S.sbuf("lnmean", [128, NO], F32, st)
            rt_ = S.sbuf("lnrstd", [128, NO], F32, st)
            S.dma("sp", mt[:, :], lnm.ap[:, :], reads=[lnm], writes=[mt])
            S.dma("sp", rt_[:, :], lnr.ap[:, :], reads=[lnr], writes=[rt_])
            gl = S.sbuf("gln", [128, DC], F32, st)
            bl = S.sbuf("bln", [128, DC], F32, st)
            S.dma("sp", gl[:, :], I["g_ln_l"][:, :], reads=[I["g_ln_l"]], writes=[gl])
            S.dma("sp", bl[:, :], I["b_ln_l"][:, :], reads=[I["b_ln_l"]], writes=[bl])
            vv = [S.sbuf(f"vv{i}", [128, NO], F32, st) for i in range(2)]
            so = [S.sbuf(f"so{i}", [128, NO], BF16, st) for i in range(2)]
            for m in range(DC):
                b = m % 2
                S.dma("sp", vv[b][:, :], uT.ap[:, m, 0:NO], reads=[uT], writes=[vv[b]])
                S.op("dve", lambda e, b=b: e.tensor_tensor(out=vv[b][:, :], in0=vv[b][:, :], in1=mt[:, :], op=ALU.subtract), reads=[vv[b], mt], writes=[vv[b]])
                S.op("dve", lambda e, b=b: e.tensor_tensor(out=vv[b][:, :], in0=vv[b][:, :], in1=rt_[:, :], op=ALU.mult), reads=[vv[b], rt_], writes=[vv[b]])
                S.op("act", lambda e, b=b, m=m: e.activation(out=so[b][:, :], in_=vv[b][:, :], func=AF.Silu, bias=bl[:, m:m + 1], scale=gl[:, m:m + 1]),
                     reads=[vv[b], bl, gl], writes=[so[b]])
                S.dma("sp", hT.ap[:, m, 0:NO], so[b][:, :], reads=[so[b]], writes=[(hT, ("s", m))])
            flush()
        with ExitStack() as st:
            lin_tm_residual(st, hT, I["w_pw2"].ap, DC, modrow.ap[2:3, 2 * D:3 * D],
                            lambda tt, nb, cs: (xres, None, xres.ap[C.OWN + tt * 128:C.OWN + (tt + 1) * 128, cs]),
                            bias_row_ap=I["b_pw2"].ap[0:1, :], bias_buf=I["b_pw2"], tok0=C.OWN, ntok=NO)
            flush()
        mlp(1, C.OWN, NO)
        S.dma("sp", out.ap[:, :], xres.ap[C.OWN:C.OWN + C.RPC * 64, :], reads=[xres], writes=[out])
        S.wait_all("pool", [out])
        S.wait_all("sp", [out])
        flush()
    return nc


_NC_CACHE = {}


def _run(C, inputs):
    maps = prep_inputs(C, inputs)
    key = (C.D, C.SEQ, C.NCORES)
    if key not in _NC_CACHE:
        _NC_CACHE[key] = build(C)
    nc = _NC_CACHE[key]
    res = run_bass_kernel_spmd(nc, maps, core_ids=list(range(C.NCORES)))
    outs = [np.asarray(r["out"], np.float32) for r in res.results]
    return np.concatenate(outs, axis=0).reshape(1, C.SEQ, C.D)


def kernel(**inputs):
    return _run(Cfg(), inputs)
```
